# Optimizing a Trainium2 kernel written in Bass

```python
import math
import jax, jax.numpy as jnp
from jax import lax
import numpy as np

D_MODEL = 1024
BATCH = 16
SEQ = 4096
DEPTH = 2

CTX_LEN = 256
GRID_W = 64
N_DIFF_HEADS = 4
DIFF_HEAD_DIM = 64
DIFF_V_DIM = 2 * DIFF_HEAD_DIM
DIFF_WIDTH = N_DIFF_HEADS * DIFF_V_DIM
ROPE_FREQS = DIFF_HEAD_DIM // 4
ROPE_THETA = 10000.0
Q_BLOCK = 128
POOL_WINDOWS = (2, 4, 8, 16)
N_POOL_GROUPS = len(POOL_WINDOWS)
POOL_GROUP_DIM = D_MODEL // 8
POOL_WIDTH = N_POOL_GROUPS * POOL_GROUP_DIM
MIX_WIDTH = DIFF_WIDTH + POOL_WIDTH
MIX_IN_WIDTH = 3 * DIFF_WIDTH + POOL_WIDTH
CONV_WIDTH = 31
CONV_CH = D_MODEL
D_FF = 2816
N_MOD = 9
N_EVEN = (DEPTH + 1) // 2
N_ODD = DEPTH // 2
ALPHA = (2.0 * DEPTH) ** 0.25
BETA = (8.0 * DEPTH) ** -0.25
FFN_HALF = 0.5
LN_EPS = 1e-5

kernel_name = 'hybrid_diffattn_pool_conformer_dit'


def layer_norm(x, g, b):
    xf = x.astype(jnp.float32)
    mu = jnp.mean(xf, axis=-1, keepdims=True)
    var = jnp.mean(jnp.square(xf - mu), axis=-1, keepdims=True)
    y = (xf - mu) * lax.rsqrt(var + LN_EPS)
    return (y * g.astype(jnp.float32) + b.astype(jnp.float32)).astype(x.dtype)


def rms_norm(x, g):
    xf = x.astype(jnp.float32)
    y = xf * lax.rsqrt(jnp.mean(jnp.square(xf), axis=-1, keepdims=True) + LN_EPS)
    return (y * g.astype(jnp.float32)).astype(x.dtype)


def modulate(x, shift, scale):
    return x * (1.0 + scale) + shift


def swiglu(h, w_in, w_out):
    gate, up = jnp.split(h @ w_in, 2, axis=-1)
    return (jax.nn.silu(gate) * up) @ w_out


def axial_rope_tables(rows):
    row = jnp.repeat(jnp.arange(rows, dtype=jnp.float32), GRID_W)
    col = jnp.tile(jnp.arange(GRID_W, dtype=jnp.float32), rows)
    inv_freq = ROPE_THETA ** (-jnp.arange(ROPE_FREQS, dtype=jnp.float32) / ROPE_FREQS)
    ang = jnp.stack([row[:, None] * inv_freq, col[:, None] * inv_freq], axis=1)
    return jnp.cos(ang), jnp.sin(ang)


def apply_axial_rope(x, cos, sin):
    xs = x.reshape(x.shape[:-1] + (2, 2, ROPE_FREQS))
    x1, x2 = xs[..., 0, :], xs[..., 1, :]
    c = cos[:, None, None].astype(x.dtype)
    s = sin[:, None, None].astype(x.dtype)
    out = jnp.stack([x1 * c - x2 * s, x2 * c + x1 * s], axis=-2)
    return out.reshape(x.shape)


def diff_attention(q, k, v, lam):
    b, t = q.shape[:2]
    n_blk = t // Q_BLOCK
    scale = DIFF_HEAD_DIM ** -0.5
    q_blocks = jnp.moveaxis(q.reshape((b, n_blk, Q_BLOCK) + q.shape[2:]), 1, 0)

    def one_block(qb):
        s = jnp.einsum('bqhmd,bkhmd->bhmqk', qb, k).astype(jnp.float32) * scale
        p = jax.nn.softmax(s, axis=-1)
        p_diff = p[:, :, 0] - lam * p[:, :, 1]
        return jnp.einsum('bhqk,bkhe->bqhe', p_diff.astype(v.dtype), v)

    out = lax.map(one_block, q_blocks)
    return jnp.moveaxis(out, 0, 1).reshape(b, t, N_DIFF_HEADS, DIFF_V_DIM)


def multiscale_pool(u, w_pool, pool_scale):
    b, t, _ = u.shape
    uf = u.astype(jnp.float32)
    csum = jnp.concatenate([jnp.zeros((b, 1, POOL_WIDTH), jnp.float32), jnp.cumsum(uf, axis=1)], axis=1)
    pos = jnp.arange(t)
    groups = []
    for g, win in enumerate(POOL_WINDOWS):
        lo = jnp.clip(pos - win // 2, 0, t)
        hi = jnp.clip(pos - win // 2 + win, 0, t)
        sl = slice(g * POOL_GROUP_DIM, (g + 1) * POOL_GROUP_DIM)
        cs = csum[..., sl]
        cnt = (hi - lo).astype(jnp.float32)[None, :, None]
        mean = (jnp.take(cs, hi, axis=1) - jnp.take(cs, lo, axis=1)) / cnt
        groups.append(mean - uf[..., sl])
    pooled = jnp.stack(groups, axis=2).astype(u.dtype)
    y = jnp.einsum('btgc,gcd->btgd', pooled, w_pool).reshape(b, t, POOL_WIDTH)
    return y * pool_scale


def conv_module(h, w_c1, b_c1, w_dw, b_dw, ln_g, ln_b, w_c2, b_c2):
    a, g = jnp.split(h @ w_c1 + b_c1, 2, axis=-1)
    z = a * jax.nn.sigmoid(g)
    z = lax.conv_general_dilated(
        z, w_dw[:, None, :], window_strides=(1,),
        padding=[(CONV_WIDTH // 2, CONV_WIDTH // 2)],
        dimension_numbers=('NWC', 'WIO', 'NWC'), feature_group_count=CONV_CH) + b_dw
    z = jax.nn.silu(layer_norm(z, ln_g, ln_b))
    return z @ w_c2 + b_c2


def even_mixer(h_lat, h_ctx, cos, sin, w_in, w_out, lam_q1, lam_k1, lam_q2, lam_k2, subln_g,
               w_pool, pool_scale, lam_init, ctx_out):
    f32 = jnp.float32
    lam = (jnp.exp(jnp.sum(lam_q1.astype(f32) * lam_k1.astype(f32)))
           - jnp.exp(jnp.sum(lam_q2.astype(f32) * lam_k2.astype(f32))) + lam_init)

    def heads(z, tail):
        return z.reshape(z.shape[:2] + (N_DIFF_HEADS,) + tail)

    z_lat = h_lat @ w_in
    q_lat = apply_axial_rope(heads(z_lat[..., :DIFF_WIDTH], (2, DIFF_HEAD_DIM)), cos, sin)
    k_lat = apply_axial_rope(heads(z_lat[..., DIFF_WIDTH:2 * DIFF_WIDTH], (2, DIFF_HEAD_DIM)), cos, sin)
    v_lat = heads(z_lat[..., 2 * DIFF_WIDTH:3 * DIFF_WIDTH], (DIFF_V_DIM,))
    u_lat = z_lat[..., 3 * DIFF_WIDTH:]

    if ctx_out:
        z_ctx = h_ctx @ w_in
        kv_ctx = z_ctx[..., DIFF_WIDTH:3 * DIFF_WIDTH]
    else:
        kv_ctx = h_ctx @ w_in[:, DIFF_WIDTH:3 * DIFF_WIDTH]
    k_ctx = heads(kv_ctx[..., :DIFF_WIDTH], (2, DIFF_HEAD_DIM))
    v_ctx = heads(kv_ctx[..., DIFF_WIDTH:], (DIFF_V_DIM,))
    k_all = jnp.concatenate([k_ctx, k_lat], axis=1)
    v_all = jnp.concatenate([v_ctx, v_lat], axis=1)

    def merge(o, u):
        o = rms_norm(o, subln_g) * (1.0 - lam_init)
        o = o.reshape(o.shape[:2] + (DIFF_WIDTH,))
        return jnp.concatenate([o, multiscale_pool(u, w_pool, pool_scale)], axis=-1) @ w_out

    y_lat = merge(diff_attention(q_lat, k_all, v_all, lam), u_lat)
    y_ctx = None
    if ctx_out:
        q_ctx = heads(z_ctx[..., :DIFF_WIDTH], (2, DIFF_HEAD_DIM))
        y_ctx = merge(diff_attention(q_ctx, k_ctx, v_ctx, lam), z_ctx[..., 3 * DIFF_WIDTH:])
    return y_lat, y_ctx


def setup_inputs(seed: int = 0) -> dict:
    key = jax.random.key(seed)
    ks = jax.random.split(key, 32)
    D = D_MODEL

    def nrm(k, shape, s):
        return jax.random.normal(k, shape, jnp.float32) * s

    return {
        'x': nrm(ks[0], (BATCH, SEQ, D), 1.0),
        'c': nrm(ks[1], (BATCH, D), 1.0),
        'ctx': nrm(ks[2], (BATCH, CTX_LEN, D), 1.0),
        'c_ctx': nrm(ks[3], (D,), 1.0),
        'w_ada': nrm(ks[4], (DEPTH, D, N_MOD * D), D ** -0.5),
        'b_ada': nrm(ks[5], (DEPTH, N_MOD * D), 0.01),
        'ln_g': 1.0 + nrm(ks[6], (DEPTH, 3, D), 0.02),
        'ln_b': nrm(ks[7], (DEPTH, 3, D), 0.02),
        'w_ffn_in': nrm(ks[8], (DEPTH, 2, D, 2 * D_FF), D ** -0.5),
        'w_ffn_out': nrm(ks[9], (DEPTH, 2, D_FF, D), BETA * D_FF ** -0.5),
        'w_mix_in': nrm(ks[10], (N_EVEN, D, MIX_IN_WIDTH), D ** -0.5),
        'w_mix_out': nrm(ks[11], (N_EVEN, MIX_WIDTH, D), BETA * MIX_WIDTH ** -0.5),
        'lam_q1': nrm(ks[12], (N_EVEN, DIFF_HEAD_DIM), 0.1),
        'lam_k1': nrm(ks[13], (N_EVEN, DIFF_HEAD_DIM), 0.1),
        'lam_q2': nrm(ks[14], (N_EVEN, DIFF_HEAD_DIM), 0.1),
        'lam_k2': nrm(ks[15], (N_EVEN, DIFF_HEAD_DIM), 0.1),
        'subln_g': 1.0 + nrm(ks[16], (N_EVEN, DIFF_V_DIM), 0.02),
        'w_pool': nrm(ks[17], (N_EVEN, N_POOL_GROUPS, POOL_GROUP_DIM, POOL_GROUP_DIM), POOL_GROUP_DIM ** -0.5),
        'pool_scale': 1.0 + nrm(ks[18], (N_EVEN, POOL_WIDTH), 0.1),
        'w_c1': nrm(ks[19], (N_ODD, D, 2 * CONV_CH), D ** -0.5),
        'b_c1': nrm(ks[20], (N_ODD, 2 * CONV_CH), 0.01),
        'w_dw': nrm(ks[21], (N_ODD, CONV_WIDTH, CONV_CH), CONV_WIDTH ** -0.5),
        'b_dw': nrm(ks[22], (N_ODD, CONV_CH), 0.01),
        'conv_ln_g': 1.0 + nrm(ks[23], (N_ODD, CONV_CH), 0.02),
        'conv_ln_b': nrm(ks[24], (N_ODD, CONV_CH), 0.02),
        'w_c2': nrm(ks[25], (N_ODD, CONV_CH, D), BETA * CONV_CH ** -0.5),
        'b_c2': nrm(ks[26], (N_ODD, D), 0.01),
    }


def reference(x, c, ctx, c_ctx, w_ada, b_ada, ln_g, ln_b, w_ffn_in, w_ffn_out, w_mix_in, w_mix_out,
              lam_q1, lam_k1, lam_q2, lam_k2, subln_g, w_pool, pool_scale,
              w_c1, b_c1, w_dw, b_dw, conv_ln_g, conv_ln_b, w_c2, b_c2):
    rows = x.shape[1] // GRID_W
    cos, sin = axial_rope_tables(rows)
    silu_c = jax.nn.silu(c)
    silu_cc = jax.nn.silu(c_ctx)
    h_lat, h_ctx = x, ctx
    for l in range(DEPTH):
        even = l % 2 == 0
        ctx_later = any(j % 2 == 0 for j in range(l + 1, DEPTH))
        ctx_here = even or ctx_later
        m_lat = [m[:, None, :] for m in jnp.split(silu_c @ w_ada[l] + b_ada[l], N_MOD, axis=-1)]
        m_ctx = [m[None, None, :] for m in jnp.split(silu_cc @ w_ada[l] + b_ada[l], N_MOD, axis=-1)]

        def sublayer(h, m, i, fn, res_w):
            y = fn(modulate(h, m[3 * i], m[3 * i + 1]))
            return layer_norm(ALPHA * h + res_w * m[3 * i + 2] * y, ln_g[l, i], ln_b[l, i])

        ffn1 = lambda h: swiglu(h, w_ffn_in[l, 0], w_ffn_out[l, 0])
        ffn2 = lambda h: swiglu(h, w_ffn_in[l, 1], w_ffn_out[l, 1])

        h_lat = sublayer(h_lat, m_lat, 0, ffn1, FFN_HALF)
        if ctx_here:
            h_ctx = sublayer(h_ctx, m_ctx, 0, ffn1, FFN_HALF)

        x_lat = modulate(h_lat, m_lat[3], m_lat[4])
        if even:
            e = l // 2
            x_ctx = modulate(h_ctx, m_ctx[3], m_ctx[4])
            y_lat, y_ctx = even_mixer(
                x_lat, x_ctx, cos, sin, w_mix_in[e], w_mix_out[e], lam_q1[e], lam_k1[e], lam_q2[e], lam_k2[e],
                subln_g[e], w_pool[e], pool_scale[e], 0.8 - 0.6 * math.exp(-0.3 * l), ctx_later)
        else:
            o = l // 2
            conv = lambda h: conv_module(h, w_c1[o], b_c1[o], w_dw[o], b_dw[o], conv_ln_g[o], conv_ln_b[o],
                                         w_c2[o], b_c2[o])
            y_lat = conv(x_lat)
            y_ctx = conv(modulate(h_ctx, m_ctx[3], m_ctx[4])) if ctx_later else None
        h_lat = layer_norm(ALPHA * h_lat + m_lat[5] * y_lat, ln_g[l, 1], ln_b[l, 1])
        if ctx_later:
            h_ctx = layer_norm(ALPHA * h_ctx + m_ctx[5] * y_ctx, ln_g[l, 1], ln_b[l, 1])

        h_lat = sublayer(h_lat, m_lat, 2, ffn2, FFN_HALF)
        if ctx_later:
            h_ctx = sublayer(h_ctx, m_ctx, 2, ffn2, FFN_HALF)
    return h_lat
```

```python
import math
from contextlib import ExitStack
import numpy as np
import concourse.bass as bass
import concourse.mybir as mybir
from concourse.bass_utils import run_bass_kernel_spmd

F32 = mybir.dt.float32
BF16 = mybir.dt.bfloat16
AF = mybir.ActivationFunctionType
ALU = mybir.AluOpType

D = 1024
DFF = 2816
NFC = 22
DEPTH = 2
ALPHA = (2.0 * DEPTH) ** 0.25
EPS = 1e-5
CTX = 256
TS = 512
POOL_W = (2, 4, 8, 16)
CW = 31
LAM_INIT0 = 0.8 - 0.6 * math.exp(-0.3 * 0)


class Sem:
    def __init__(self, h):
        self.h = h
        self.n = 0


class Reg:
    __slots__ = ("w", "r")

    def __init__(self):
        self.w = None
        self.r = {}


class Prog:
    ENG = ("pe", "act", "dve", "pool", "sp")

    def __init__(self, nc, sem_handles):
        self.nc = nc
        self.free = list(sem_handles)
        self.q = {e: [] for e in self.ENG}
        self.waited = {e: {} for e in self.ENG}
        self.pend = {e: [] for e in self.ENG}
        self.esem = {}
        self.allsems = []
        for e in ("pe", "act", "dve"):
            self.esem[e] = self.new_sem()

    def new_sem(self):
        s = Sem(self.free.pop())
        self.allsems.append(s)
        return s

    def op(self, eng, fn, reads=(), writes=(), dsem=None, signal=True):
        wt = self.waited[eng]
        waits = {}
        own = dsem if dsem is not None else self.esem[eng]

        def need(tok):
            if tok is None:
                return
            s, v = tok
            if dsem is not None and s is dsem:
                return
            if eng == "pe" and s is own:
                return
            if wt.get(s, 0) >= v:
                return
            if waits.get(s, 0) < v:
                waits[s] = v

        for r in reads:
            need(r.w)
        for w in writes:
            need(w.w)
            for s, v in w.r.items():
                need((s, v))
        for s, v in waits.items():
            wt[s] = v
        wl = list(waits.items())
        if not signal:
            self.pend[eng].extend(reads)
            self.q[eng].append((fn, wl, None, 0))
            return None
        inc = 16 if dsem is not None else 1
        own.n += inc
        tok = (own, own.n)
        for r in list(reads) + self.pend[eng]:
            if r.r.get(own, 0) < own.n:
                r.r[own] = own.n
        self.pend[eng] = []
        for w in writes:
            w.w = tok
            w.r = {}
        self.q[eng].append((fn, wl, own, inc))
        return tok

    def barrier(self):
        for e in self.ENG:
            assert not self.pend[e]
            for s in self.allsems:
                if s.n > 0 and self.waited[e].get(s, 0) < s.n:
                    self.q[e].append((None, [(s, s.n)], None, 0))
                    self.waited[e][s] = s.n
        for e in ("pe", "act", "dve"):
            self.esem[e] = self.new_sem()

    def replay(self, eng, h):
        for fn, waits, sem, inc in self.q[eng]:
            for s, v in waits:
                h.wait_ge(s.h, v)
            if fn is None:
                continue
            ins = fn(h)
            if sem is not None:
                ins.then_inc(sem.h, inc)


class T:
    def __init__(self, t, nreg=1):
        self.t = t
        self.regs = [Reg() for _ in range(nreg)]

    @property
    def r(self):
        return self.regs[0]


def build(NB, TT, debug=False):
    NTL = NB * TT // TS
    TPB = TT // TS
    NTILE = NTL + 1
    NKC = (CTX + TT) // 128
    NKEY = CTX + TT
    nc = bass.Bass("TRN2", target_bir_lowering=False)

    def din(name, shape, dt=F32):
        return nc.dram_tensor(name, list(shape), dt, kind="ExternalInput").ap()

    x_d = din("x", [NB * TT, D])
    ctx_d = din("ctx", [NB * CTX, D])
    cc_d = din("cc", [NB + 1, D])
    w_ada = din("w_ada", [2, D, 9 * D])
    b_ada = din("b_ada", [2, 9 * D])
    ln_g = din("ln_g", [2, 3, D])
    ln_b = din("ln_b", [2, 3, D])
    w_ffn_in = din("w_ffn_in", [2, 2, D, 2 * DFF])
    w_ffn_out = din("w_ffn_out", [2, 2, DFF, D])
    w_mix_in = din("w_mix_in", [1, D, 2048])
    w_mix_out = din("w_mix_out", [1, D, D])
    lam_d = {k: din(k, [1, 64]) for k in ("lam_q1", "lam_k1", "lam_q2", "lam_k2")}
    subln_g = din("subln_g", [1, 128])
    w_pool = din("w_pool", [1, 4, 128, 128])
    pool_scale = din("pool_scale", [1, 512])
    w_c1 = din("w_c1", [1, D, 2 * D])
    b_c1 = din("b_c1", [1, 2 * D])
    w_dw = din("w_dw", [1, CW, D])
    b_dw = din("b_dw", [1, D])
    conv_ln_g = din("conv_ln_g", [1, D])
    conv_ln_b = din("conv_ln_b", [1, D])
    w_c2 = din("w_c2", [1, D, D])
    b_c2 = din("b_c2", [1, D])
    ident_d = din("ident", [128, 128])
    cos_d = din("rope_cos", [128, TT])
    sin_d = din("rope_sin", [128, TT])
    rcnt_d = din("rcnt", [128, 4, TT])
    out_d = nc.dram_tensor("out", [NB * TT, D], F32, kind="ExternalOutput").ap()

    def dscr(name, shape, dt):
        kind = "ExternalOutput" if debug else "Internal"
        return nc.dram_tensor(name, list(shape), dt, kind=kind).ap()

    HA = [dscr(f"ha{i}", [NTILE, 128, 8 * TS], F32) for i in range(2)]
    HMID = dscr("hmid", [NTILE, 128, NFC * TS], BF16)
    QT_d = dscr("qt", [NTL, 128, 4 * TS], BF16)
    KT_d = dscr("kt", [NB, 128, 4, NKEY], BF16)
    V_d = dscr("v", [NB, 128, NKC, 512], BF16)
    U_d = dscr("u", [NB, 128, 4, TT + 16], F32)
    ZC_d = dscr("zc", [NB, 128, 8, TT + 32], BF16)
    MG_d = dscr("mgd", [NTL, 128, 8 * TS], BF16) if debug else None

    es = ExitStack()
    with es:
        nsem = 100
        sems = [es.enter_context(nc.semaphore(f"s{i}")) for i in range(nsem)]
        P = Prog(nc, sems)
        cnt = [0]

        def sb(stack, shape, dt, nreg=1):
            cnt[0] += 1
            return T(stack.enter_context(nc.sbuf_tensor(f"sb{cnt[0]}", list(shape), dt)), nreg)

        PS = [T(es.enter_context(nc.psum_tensor(f"ps{i}", [128, 512], F32))) for i in range(8)]

        ident = sb(es, [128, 128], F32)
        onesb = sb(es, [128, 128], BF16)
        ones128 = sb(es, [128, 128], BF16)
        ones1 = sb(es, [128, 128], BF16)
        MOD = [sb(es, [128, 9, 8, 3], F32) for _ in range(2)]
        colsA = sb(es, [128, 120], F32)
        colsB = sb(es, [128, 120], F32)
        colsC = sb(es, [128, 64], F32)
        colsD = sb(es, [128, 124], F32)
        colsE = sb(es, [128, 124], F32)
        AIN = sb(es, [128, 2, 3, 8, 3], F32)
        GAT = sb(es, [128, 2, 3, 8, 3], F32)
        LNGA = sb(es, [128, 48], F32)
        LNBA = sb(es, [128, 48], F32)
        lamc = sb(es, [128, 4], F32)
        sublc = sb(es, [128, 1], F32)
        bc2g = sb(es, [128, 8, 3], F32)
        dsem_ld = [P.new_sem() for _ in range(18)]
        dsem_st = [P.new_sem() for _ in range(12)]

        def mm(out, out_ap, lhsT, rhs, start, stop, reads):
            P.op("pe", lambda e: e.matmul(out_ap, lhsT=lhsT, rhs=rhs, start=start, stop=stop),
                 reads=reads, writes=[out.r], signal=stop)

        def act(out_reg, out_ap, in_ap, func, reads, bias=None, scale=None):
            kw = {}
            if bias is not None:
                kw["bias"] = bias
            if scale is not None:
                kw["scale"] = scale
            return P.op("act", lambda e: e.activation(out=out_ap, in_=in_ap, func=func, **kw),
                        reads=reads, writes=[out_reg])

        def dve(fn, reads, writes):
            return P.op("dve", fn, reads=reads, writes=writes)

        def load(dst_reg, dst_ap, src_ap, sem, reads=()):
            wl = list(dst_reg) if isinstance(dst_reg, (list, tuple)) else [dst_reg]
            return P.op("sp", lambda e: e.dma_start(out=dst_ap, in_=src_ap), reads=reads,
                        writes=wl, dsem=sem)

        def store(src_reg, dst_ap, src_ap, sem):
            rl = list(src_reg) if isinstance(src_reg, (list, tuple)) else [src_reg]
            return P.op("pool", lambda e: e.dma_start(out=dst_ap, in_=src_ap), reads=rl,
                        writes=[], dsem=sem)

        castrr = [0]

        def cast(out_reg, out_ap, in_ap, reads):
            castrr[0] += 1
            if castrr[0] % 2:
                act(out_reg, out_ap, in_ap, AF.Copy, reads)
            else:
                dve(lambda e: e.tensor_copy(out=out_ap, in_=in_ap), reads, [out_reg])

        STG_N = 2816

        def load_w(stack_stg, dst, src, K, N, col0=0, ncols=None, sems=None, stg_n=STG_N):
            ncols = ncols or N
            KC = K // 128
            cb = min(ncols, stg_n)
            per = max(1, min(KC, stg_n // cb))
            sems = sems or dsem_ld[8:10]
            i = 0
            for c0 in range(0, ncols, cb):
                c1 = min(ncols, c0 + cb)
                for k0 in range(0, KC, per):
                    k1 = min(KC, k0 + per)
                    st = stack_stg[i % 2]
                    i += 1
                    sap = st.t[:, 0:(k1 - k0) * (c1 - c0)].rearrange("p (c n) -> p c n", n=c1 - c0)
                    srcap = src[k0 * 128:k1 * 128, col0 + c0:col0 + c1].rearrange("(c p) n -> p c n", p=128)
                    load(st.r, sap, srcap, sems[i % 2])
                    cast(dst.r, dst.t[:, k0:k1, c0:c1], sap, [st.r])

        def load_w_blocks(stack_stg, dst, src, K, blocks):
            KC = K // 128
            wsems = [dsem_ld[8], dsem_ld[9], dsem_ld[14], dsem_ld[15]]
            slots = []
            for st in stack_stg:
                slots.append((st, 0))
            width = STG_N
            need = max(KC * nc_ for blk in blocks for _, nc_ in blk)
            if need * 2 <= STG_N:
                slots = [(st, o) for st in stack_stg for o in (0, STG_N // 2)]
                width = STG_N // 2
            regs = {}
            for st, o in slots:
                regs[(id(st), o)] = Reg()
            i = 0
            for r, blk in enumerate(blocks):
                for c0, nc_ in blk:
                    st, o = slots[i % len(slots)]
                    rg = regs[(id(st), o)]
                    sem = wsems[i % len(slots)]
                    i += 1
                    sap = st.t[:, o:o + KC * nc_].rearrange("p (c n) -> p c n", n=nc_)
                    srcap = src[:, c0:c0 + nc_].rearrange("(c p) n -> p c n", p=128)
                    load(rg, sap, srcap, sem)
                    cast(dst.regs[r], dst.t[:, :, c0:c0 + nc_], sap, [rg])

        with ExitStack() as ph:
            stg = [sb(ph, [128, STG_N], F32) for _ in range(2)]
            rows = [sb(ph, [128, 128], F32) for _ in range(5)]
            ccs = sb(ph, [4, D], F32)
            scT = sb(ph, [128, 8, 4], BF16)
            wblk = [sb(ph, [128, 8, 1024], BF16) for _ in range(2)]
            lamt = sb(ph, [128, 4, 64], F32)
            load(ident.r, ident.t[:], ident_d, dsem_ld[0])
            dve(lambda e: e.memset(onesb.t[:], 1.0 / 1024), [], [onesb.r])
            dve(lambda e: e.memset(ones128.t[:], 1.0 / 128), [], [ones128.r])
            dve(lambda e: e.memset(ones1.t[:], 1.0), [], [ones1.r])
            for r_ in rows:
                dve(lambda e, r_=r_: e.memset(r_.t[:], 0.0), [], [r_.r])

            def vrow(rt, r0, vec_ap, n):
                load(rt.r, rt.t[r0:r0 + n, :], vec_ap, dsem_ld[1])

            vrow(rows[0], 0, b_ada[0].rearrange("(c p) -> c p", p=128), 72)
            vrow(rows[0], 72, ln_g.rearrange("l i (c p) -> (l i c) p", p=128), 48)
            vrow(rows[1], 0, b_ada[1].rearrange("(c p) -> c p", p=128), 72)
            vrow(rows[1], 72, ln_b.rearrange("l i (c p) -> (l i c) p", p=128), 48)
            o = 0
            for v_ap, n in ((pool_scale[0], 4), (b_c1[0], 16), (b_dw[0], 8), (conv_ln_g[0], 8),
                            (conv_ln_b[0], 8), (b_c2[0], 8), (subln_g[0], 1)):
                vrow(rows[2], o, v_ap.rearrange("(c p) -> c p", p=128), n)
                o += n
            wdw_rows = w_dw[0].rearrange("j (c p) -> (j c) p", p=128)
            vrow(rows[3], 0, wdw_rows[0:124, :], 124)
            vrow(rows[4], 0, wdw_rows[124:248, :], 124)
            for rt in rows:
                rt.r.w = (dsem_ld[1], dsem_ld[1].n)
            for rt, ct, n in ((rows[0], colsA, 120), (rows[1], colsB, 120), (rows[2], colsC, 64),
                              (rows[3], colsD, 124), (rows[4], colsE, 124)):
                P.op("pe", lambda e, rt=rt, n=n: e.transpose(out=PS[0].t[:, 0:n], in_=rt.t[0:n, :],
                                                               identity=ident.t[0:n, 0:n]),
                     reads=[rt.r, ident.r], writes=[PS[0].r])
                dve(lambda e, ct=ct, n=n: e.tensor_copy(out=ct.t[:, 0:n], in_=PS[0].t[:, 0:n]),
                    [PS[0].r], [ct.r])
            NR = NB + 1
            load(ccs.r, ccs.t[0:NR, :], cc_d, dsem_ld[2])
            act(ccs.r, ccs.t[0:NR, :], ccs.t[0:NR, :], AF.Silu, [ccs.r])
            for k in range(8):
                P.op("pe", lambda e, k=k: e.transpose(out=PS[1].t[:, k * 4:k * 4 + NR],
                                                       in_=ccs.t[0:NR, k * 128:(k + 1) * 128],
                                                       identity=ident.t[0:NR, 0:NR]),
                     reads=[ccs.r, ident.r], writes=[PS[1].r])
            dve(lambda e: e.memset(scT.t[:], 0.0), [], [scT.r])
            dve(lambda e: e.tensor_copy(out=scT.t[:, :, 0:NR],
                                        in_=PS[1].t[:, 0:32].rearrange("p (k r) -> p k r", r=4)[:, :, 0:NR]),
                [PS[1].r], [scT.r])
            for l in range(2):
                bcols = colsA if l == 0 else colsB
                for jb in range(9):
                    wb = wblk[jb % 2]
                    load_w(stg, wb, w_ada[l], D, 9 * D, col0=jb * 1024, ncols=1024)
                    pst = PS[2 + jb % 2]
                    for oc in range(8):
                        for k in range(8):
                            mm(pst, pst.t[:, oc * 4:oc * 4 + NR], wb.t[:, k, oc * 128:(oc + 1) * 128],
                               scT.t[:, k, 0:NR], k == 0, k == 7, [wb.r, scT.r])
                    for r_ in range(NR):
                        dve(lambda e, l=l, jb=jb, r_=r_, pst=pst, bcols=bcols: e.tensor_tensor(
                            out=MOD[l].t[:, jb, :, r_],
                            in0=pst.t[:, 0:32].rearrange("p (c r) -> p c r", r=4)[:, :, r_],
                            in1=bcols.t[:, jb * 8:(jb + 1) * 8], op=ALU.add),
                            [pst.r, bcols.r], [MOD[l].r])
            for l in range(2):
                for i in range(3):
                    dve(lambda e, l=l, i=i: e.tensor_scalar(out=AIN.t[:, l, i], in0=MOD[l].t[:, 3 * i + 1],
                                                            scalar1=1.0, scalar2=1.0 / ALPHA,
                                                            op0=ALU.add, op1=ALU.mult),
                        [MOD[l].r], [AIN.r])
                    rw = 1.0 if i == 1 else 0.5
                    dve(lambda e, l=l, i=i, rw=rw: e.tensor_scalar(out=GAT.t[:, l, i], in0=MOD[l].t[:, 3 * i + 2],
                                                                   scalar1=rw, scalar2=None, op0=ALU.mult),
                        [MOD[l].r], [GAT.r])
            dve(lambda e: e.tensor_scalar(out=LNGA.t[:], in0=colsA.t[:, 72:120], scalar1=ALPHA, scalar2=None,
                                          op0=ALU.mult), [colsA.r], [LNGA.r])
            dve(lambda e: e.tensor_scalar(out=LNBA.t[:], in0=colsB.t[:, 72:120], scalar1=ALPHA, scalar2=None,
                                          op0=ALU.mult), [colsB.r], [LNBA.r])
            for r_ in range(NR):
                dve(lambda e, r_=r_: e.tensor_tensor(out=bc2g.t[:, :, r_], in0=GAT.t[:, 1, 1, :, r_],
                                                     in1=colsC.t[:, 44:52], op=ALU.mult),
                    [GAT.r, colsC.r], [bc2g.r])
            dve(lambda e: e.tensor_scalar(out=sublc.t[:], in0=colsC.t[:, 52:53], scalar1=1.0 - LAM_INIT0,
                                          scalar2=None, op0=ALU.mult), [colsC.r], [sublc.r])
            for j, k in enumerate(("lam_q1", "lam_k1", "lam_q2", "lam_k2")):
                load(lamt.r, lamt.t[:, j, :], lam_d[k].partition_broadcast(128), dsem_ld[3])
            dve(lambda e: e.tensor_tensor(out=lamt.t[:, 0, :], in0=lamt.t[:, 0, :], in1=lamt.t[:, 1, :],
                                          op=ALU.mult), [lamt.r], [lamt.r])
            dve(lambda e: e.tensor_tensor(out=lamt.t[:, 2, :], in0=lamt.t[:, 2, :], in1=lamt.t[:, 3, :],
                                          op=ALU.mult), [lamt.r], [lamt.r])
            dve(lambda e: e.reduce_sum(out=lamc.t[:, 1:2], in_=lamt.t[:, 0, :], axis=mybir.AxisListType.X),
                [lamt.r], [lamc.r])
            dve(lambda e: e.reduce_sum(out=lamc.t[:, 2:3], in_=lamt.t[:, 2, :], axis=mybir.AxisListType.X),
                [lamt.r], [lamc.r])
            act(lamc.r, lamc.t[:, 1:3], lamc.t[:, 1:3], AF.Exp, [lamc.r])
            dve(lambda e: e.scalar_tensor_tensor(out=lamc.t[:, 0:1], in0=lamc.t[:, 2:3], scalar=-LAM_INIT0,
                                                 in1=lamc.t[:, 1:2], op0=ALU.add, op1=ALU.subtract),
                [lamc.r], [lamc.r])
            P.barrier()

        def mod_row(t):
            return NB if t == NTL else t // TPB

        def ln_cast(xap, xreg, c, xb, xsq):
            sl = c % 2
            act(xb.regs[sl], xb.t[:, sl, :], xap, AF.Copy, [xreg])
            act(xsq.regs[sl], xsq.t[:, sl, :], xap, AF.Square, [xreg])

        def ln_mm(c, xb, xsq, psS, psQ):
            sl = c % 2
            mm(psS, psS.t[:], onesb.t[:], xb.t[:, sl, :], c == 0, c == 7, [onesb.r, xb.regs[sl]])
            mm(psQ, psQ.t[:], onesb.t[:], xsq.t[:, sl, :], c == 0, c == 7, [onesb.r, xsq.regs[sl]])

        def ln_stats(xap, xreg, c, xb, xsq, psS, psQ):
            ln_cast(xap, xreg, c, xb, xsq)
            if c > 0:
                ln_mm(c - 1, xb, xsq, psS, psQ)
            if c == 7:
                ln_mm(7, xb, xsq, psS, psQ)

        def ln_finish(psS, psQ, mean, rstd):
            act(mean.r, mean.t[:], psS.t[:], AF.Copy, [psS.r])
            act(rstd.r, rstd.t[:], psS.t[:], AF.Square, [psS.r])
            dve(lambda e: e.scalar_tensor_tensor(out=rstd.t[:], in0=psQ.t[:], scalar=EPS, in1=rstd.t[:],
                                                 op0=ALU.add, op1=ALU.subtract), [psQ.r, rstd.r], [rstd.r])
            act(rstd.r, rstd.t[:], rstd.t[:], AF.Ln, [rstd.r])
            act(rstd.r, rstd.t[:], rstd.t[:], AF.Exp, [rstd.r], scale=-0.5)

        def ln_norm(xap, xreg, mean, rstd):
            dve(lambda e: e.tensor_tensor(out=xap, in0=xap, in1=mean.t[:], op=ALU.subtract),
                [xreg, mean.r], [xreg])
            dve(lambda e: e.tensor_tensor(out=xap, in0=xap, in1=rstd.t[:], op=ALU.mult),
                [xreg, rstd.r], [xreg])

        def tile_src(t):
            return (x_d[t * TS:(t + 1) * TS, :] if t < NTL else ctx_d[:, :]).rearrange("(s p) d -> p s d", p=128)

        def phase_ffn_a(l, i, src_ha, dst_ha=None, tiles=None):
            tiles = list(range(NTL)) if tiles is None else tiles
            with ExitStack() as ph:
                stg = [sb(ph, [128, STG_N], F32) for _ in range(2)]
                W = sb(ph, [128, 8, 2 * DFF], BF16, NFC)
                hA = [sb(ph, [128, 8, TS], F32, 8) for _ in range(2)]
                xin = [sb(ph, [128, 4, D], F32) for _ in range(1)] if src_ha is None else None
                xm = [sb(ph, [128, 8, TS], BF16, 8) for _ in range(2)]
                sg = [sb(ph, [128, TS], F32) for _ in range(2)]
                hm = [sb(ph, [128, 2, TS], BF16) for _ in range(3)]
                hmi = 0

                def prep_load(ti):
                    if src_ha is None or ti >= len(tiles):
                        return
                    t = tiles[ti]
                    h = hA[ti % 2]
                    load(h.regs, h.t[:].rearrange("p c n -> p (c n)"), src_ha[t], dsem_ld[ti % 2])

                def prep(ti):
                    t = tiles[ti]
                    s = ti % 2
                    r_ = mod_row(t)
                    h = hA[s]
                    if src_ha is None:
                        xi = xin[0]
                        load(xi.r, xi.t[:], tile_src(t), dsem_ld[0])
                        for c in range(8):
                            pst = PS[4 + c % 2]
                            for s4 in range(4):
                                P.op("pe", lambda e, c=c, s4=s4, pst=pst, xi=xi: e.transpose(
                                    out=pst.t[:, s4 * 128:(s4 + 1) * 128], in_=xi.t[:, s4, c * 128:(c + 1) * 128],
                                    identity=ident.t[:]), reads=[xi.r, ident.r], writes=[pst.r])
                            act(h.regs[c], h.t[:, c, :], pst.t[:], AF.Copy, [pst.r], scale=ALPHA)
                        store(h.regs, dst_ha[t], h.t[:].rearrange("p c n -> p (c n)"), dsem_st[s])
                    for c in range(8):
                        act(xm[s].regs[c], xm[s].t[:, c, :], h.t[:, c, :], AF.Identity, [h.regs[c], AIN.r, MOD[l].r],
                            bias=MOD[l].t[:, 3 * i, c, r_:r_ + 1], scale=AIN.t[:, l, i, c, r_:r_ + 1])

                prep_load(0)
                prep_load(1)
                prep(0)
                load_w_blocks(stg, W, w_ffn_in[l, i // 2], D,
                              [[(f * 128, 128), (DFF + f * 128, 128)] for f in range(NFC)])
                for ti, t in enumerate(tiles):
                    s = ti % 2
                    for f in range(NFC):
                        pg, pu = PS[f % 2], PS[2 + f % 2]
                        for k in range(8):
                            mm(pg, pg.t[:], W.t[:, k, f * 128:(f + 1) * 128], xm[s].t[:, k, :], k == 0, k == 7,
                               [W.regs[f], xm[s].regs[k]])
                        for k in range(8):
                            mm(pu, pu.t[:], W.t[:, k, DFF + f * 128:DFF + (f + 1) * 128], xm[s].t[:, k, :],
                               k == 0, k == 7, [W.regs[f], xm[s].regs[k]])
                        sgt = sg[f % 2]
                        act(sgt.r, sgt.t[:], pg.t[:], AF.Silu, [pg.r])
                        if f == 10 and ti + 1 < len(tiles):
                            prep(ti + 1)
                            prep_load(ti + 2)
                        hmt = hm[hmi % 3]
                        dve(lambda e, hmt=hmt, f=f, sgt=sgt, pu=pu: e.tensor_tensor(
                            out=hmt.t[:, f % 2, :], in0=sgt.t[:], in1=pu.t[:], op=ALU.mult),
                            [sgt.r, pu.r], [hmt.r])
                        if f % 2 == 1:
                            store(hmt.r, HMID[t][:, (f - 1) * TS:(f + 1) * TS],
                                  hmt.t[:].rearrange("p c n -> p (c n)"), dsem_st[2 + hmi % 3])
                            hmi += 1
                P.barrier()

        def phase_ffn_b(l, i, src_ha, dst_ha, tiles=None, final=False):
            tiles = list(range(NTL)) if tiles is None else tiles
            li = (l * 3 + i) * 8
            with ExitStack() as ph:
                stg = [sb(ph, [128, STG_N], F32) for _ in range(2)]
                W2 = sb(ph, [128, NFC, D], BF16, 8)
                hA = [sb(ph, [128, 8, TS], F32, 8) for _ in range(3)]
                hmd = [sb(ph, [128, NFC, TS], BF16) for _ in range(2)]
                xb = sb(ph, [128, 2, TS], BF16, 2)
                xsq = sb(ph, [128, 2, TS], BF16, 2)
                means = [sb(ph, [128, TS], F32) for _ in range(2)]
                rstds = [sb(ph, [128, TS], F32) for _ in range(2)]
                otm = sb(ph, [128, 4, D], F32) if final else None
                tail_q = []

                def make_tail(t, h, mean, rstd, s3):
                    def chunk(d):
                        ln_norm(h.t[:, d, :], h.regs[d], mean, rstd)
                        if final:
                            act(h.regs[d], h.t[:, d, :], h.t[:, d, :], AF.Identity, [h.regs[d], colsA.r, colsB.r],
                                bias=colsB.t[:, 72 + li + d:72 + li + d + 1],
                                scale=colsA.t[:, 72 + li + d:72 + li + d + 1])
                        else:
                            act(h.regs[d], h.t[:, d, :], h.t[:, d, :], AF.Identity, [h.regs[d], LNGA.r, LNBA.r],
                                bias=LNBA.t[:, li + d:li + d + 1], scale=LNGA.t[:, li + d:li + d + 1])
                        if d < 7:
                            return
                        if final:
                            for s4 in range(4):
                                for half in range(2):
                                    pst = PS[2 + half]
                                    for c4 in range(4):
                                        c = half * 4 + c4
                                        P.op("pe", lambda e, c=c, c4=c4, s4=s4, pst=pst, h=h: e.transpose(
                                            out=pst.t[:, c4 * 128:(c4 + 1) * 128],
                                            in_=h.t[:, c, s4 * 128:(s4 + 1) * 128],
                                            identity=ident.t[:]), reads=[h.regs[c], ident.r], writes=[pst.r])
                                    act(otm.r, otm.t[:, s4, half * 512:(half + 1) * 512], pst.t[:], AF.Copy, [pst.r])
                            store(otm.r, out_d[t * TS:(t + 1) * TS, :].rearrange("(s p) d -> p s d", p=128), otm.t[:],
                                  dsem_st[9])
                        else:
                            store(h.regs, dst_ha[t], h.t[:].rearrange("p c n -> p (c n)"), dsem_st[s3])
                    return [lambda d=d: chunk(d) for d in range(8)]

                for ti, t in enumerate(tiles):
                    s = ti % 2
                    s3 = ti % 3
                    r_ = mod_row(t)
                    h = hA[s3]
                    hmt = hmd[s]
                    def tile_loads(ti_):
                        t_ = tiles[ti_]
                        hm_ = hmd[ti_ % 2]
                        h_ = hA[ti_ % 3]
                        load(hm_.r, hm_.t[:].rearrange("p c n -> p (c n)"), HMID[t_], dsem_ld[2 + ti_ % 2])
                        load(h_.regs, h_.t[:].rearrange("p c n -> p (c n)"), src_ha[t_], dsem_ld[[0, 1, 12][ti_ % 3]])
                    if ti == 0:
                        tile_loads(0)
                        load_w_blocks(stg, W2, w_ffn_out[l, i // 2], DFF, [[(d * 128, 128)] for d in range(8)])
                        if len(tiles) > 1:
                            tile_loads(1)
                    elif ti + 1 < len(tiles):
                        tile_loads(ti + 1)
                    psS, psQ = (PS[4], PS[5]) if s == 0 else (PS[6], PS[7])
                    for d in range(8):
                        py = PS[d % 2]
                        for f in range(NFC):
                            mm(py, py.t[:], W2.t[:, f, d * 128:(d + 1) * 128], hmt.t[:, f, :], f == 0, f == NFC - 1,
                               [W2.regs[d], hmt.r])
                        dve(lambda e, d=d, py=py, h=h, r_=r_: e.scalar_tensor_tensor(
                            out=h.t[:, d, :], in0=py.t[:], scalar=GAT.t[:, l, i, d, r_:r_ + 1], in1=h.t[:, d, :],
                            op0=ALU.mult, op1=ALU.add), [py.r, h.regs[d], GAT.r], [h.regs[d]])
                        ln_stats(h.t[:, d, :], h.regs[d], d, xb, xsq, psS, psQ)
                        if tail_q:
                            tail_q.pop(0)()
                    while tail_q:
                        tail_q.pop(0)()
                    ln_finish(psS, psQ, means[s], rstds[s])
                    tail_q = make_tail(t, h, means[s], rstds[s], s3)
                while tail_q:
                    tail_q.pop(0)()
                P.barrier()

        def phase_mix_a(src_ha):
            l, i = 0, 1
            with ExitStack() as ph:
                stg = [sb(ph, [128, STG_N], F32) for _ in range(2)]
                W = sb(ph, [128, 8, 2048], BF16)
                load_w(stg, W, w_mix_in[0], D, 2048)
                Wsw = sb(ph, [128, 8, 1024], BF16)
                for k in range(8):
                    for hf in range(2):
                        dve(lambda e, k=k, hf=hf: e.tensor_copy(
                            out=Wsw.t[:, k, :].rearrange("p (b h f) -> p b h f", h=2, f=16)[:, :, hf, :],
                            in_=W.t[:, k, 0:1024].rearrange("p (b h f) -> p b h f", h=2, f=16)[:, :, 1 - hf, :]),
                            [W.r], [Wsw.r])
                cos = sb(ph, [128, TT], F32)
                sin = sb(ph, [128, TT], F32)
                load(cos.r, cos.t[:], cos_d, dsem_ld[4])
                load(sin.r, sin.t[:], sin_d, dsem_ld[5])
                hA = [sb(ph, [128, 8, TS], F32, 8) for _ in range(2)]
                xm = [sb(ph, [128, 8, TS], BF16, 8) for _ in range(2)]
                qo = [sb(ph, [128, 4, TS], BF16) for _ in range(2)]
                ko = [sb(ph, [128, 4, TS], BF16) for _ in range(2)]
                vt = [sb(ph, [128, 4, 512], BF16) for _ in range(2)]
                ut = [sb(ph, [128, 4, TS], F32) for _ in range(2)]
                ta = [sb(ph, [128, TS], F32) for _ in range(2)]
                tb = [sb(ph, [128, TS], F32) for _ in range(2)]
                zt = sb(ph, [128, 4, 8], F32)
                dve(lambda e: e.memset(zt.t[:], 0.0), [], [zt.r])
                for b in range(NB):
                    store(zt.r, U_d[b][:, :, 0:8], zt.t[:], dsem_st[8])
                    store(zt.r, U_d[b][:, :, TT + 8:TT + 16], zt.t[:], dsem_st[8])
                ci = 0
                for ti, t in enumerate(range(NTILE)):
                    s = ti % 2
                    lat = t < NTL
                    r_ = mod_row(t)
                    b, tl = (t // TPB, t % TPB) if lat else (0, 0)
                    t0 = tl * TS
                    h = hA[s]
                    load(h.regs, h.t[:].rearrange("p c n -> p (c n)"), src_ha[t], dsem_ld[s])
                    for c in range(8):
                        act(xm[s].regs[c], xm[s].t[:, c, :], h.t[:, c, :], AF.Identity, [h.regs[c], AIN.r, MOD[l].r],
                            bias=MOD[l].t[:, 3 * i, c, r_:r_ + 1], scale=AIN.t[:, l, i, c, r_:r_ + 1])
                    xr = xm[s].regs
                    for qk in ((0, 1) if lat else (1,)):
                        dst = (qo if qk == 0 else ko)[s]
                        for hh in range(4):
                            c0 = qk * 512 + hh * 128
                            p1, p2 = PS[ci % 2], PS[2 + ci % 2]
                            for k in range(8):
                                mm(p1, p1.t[:], W.t[:, k, c0:c0 + 128], xm[s].t[:, k, :], k == 0, k == 7, [W.r, xr[k]])
                            if lat:
                                for k in range(8):
                                    mm(p2, p2.t[:], Wsw.t[:, k, c0:c0 + 128], xm[s].t[:, k, :], k == 0, k == 7,
                                       [Wsw.r, xr[k]])
                                a_, b_ = ta[ci % 2], tb[ci % 2]
                                dve(lambda e, a_=a_, p1=p1, t0=t0: e.tensor_tensor(
                                    out=a_.t[:], in0=p1.t[:], in1=cos.t[:, t0:t0 + TS], op=ALU.mult),
                                    [p1.r, cos.r], [a_.r])
                                dve(lambda e, b_=b_, p2=p2, t0=t0: e.tensor_tensor(
                                    out=b_.t[:], in0=p2.t[:], in1=sin.t[:, t0:t0 + TS], op=ALU.mult),
                                    [p2.r, sin.r], [b_.r])
                                dve(lambda e, a_=a_, b_=b_, dst=dst, hh=hh: e.tensor_tensor(
                                    out=dst.t[:, hh, :], in0=a_.t[:], in1=b_.t[:], op=ALU.add),
                                    [a_.r, b_.r], [dst.r])
                            else:
                                act(dst.r, dst.t[:, hh, :], p1.t[:], AF.Copy, [p1.r])
                            ci += 1
                    if lat:
                        store(qo[s].r, QT_d[t], qo[s].t[:].rearrange("p h n -> p (h n)"), dsem_st[s])
                        store(ko[s].r, KT_d[b][:, :, CTX + t0:CTX + t0 + TS], ko[s].t[:], dsem_st[2 + s])
                    else:
                        for bb in range(NB):
                            store(ko[s].r, KT_d[bb][:, :, 0:CTX], ko[s].t[:, :, bb * CTX:(bb + 1) * CTX], dsem_st[2 + s])
                    for s4 in range(4):
                        pv = PS[4 + s4 % 2]
                        for k in range(8):
                            mm(pv, pv.t[:], xm[s].t[:, k, s4 * 128:(s4 + 1) * 128], W.t[:, k, 1024:1536],
                               k == 0, k == 7, [W.r, xr[k]])
                        act(vt[s].r, vt[s].t[:, s4, :], pv.t[:], AF.Copy, [pv.r])
                    if lat:
                        store(vt[s].r, V_d[b][:, 2 + tl * 4:2 + tl * 4 + 4, :], vt[s].t[:], dsem_st[4 + s])
                        for g in range(4):
                            pu = PS[6 + g % 2]
                            for k in range(8):
                                mm(pu, pu.t[:], W.t[:, k, 1536 + g * 128:1536 + (g + 1) * 128], xm[s].t[:, k, :],
                                   k == 0, k == 7, [W.r, xr[k]])
                            act(ut[s].r, ut[s].t[:, g, :], pu.t[:], AF.Copy, [pu.r])
                        store(ut[s].r, U_d[b][:, :, 8 + t0:8 + t0 + TS], ut[s].t[:], dsem_st[6 + s])
                    else:
                        for bb in range(NB):
                            store(vt[s].r, V_d[bb][:, 0:2, :], vt[s].t[:, 2 * bb:2 * bb + 2, :], dsem_st[4 + s])
                P.barrier()

        def phase_attn(src_ha, dst_ha):
            l, i = 0, 1
            li = (l * 3 + i) * 8
            with ExitStack() as ph:
                stg = [sb(ph, [128, 1024], F32) for _ in range(2)]
                Wo = sb(ph, [128, 8, D], BF16)
                load_w(stg, Wo, w_mix_out[0], D, D, stg_n=1024)
                Wp = sb(ph, [128, 4, 128], BF16)
                load_w(stg, Wp, w_pool[0].rearrange("g c d -> (g c) d"), 512, 128, stg_n=1024)
                KT = sb(ph, [128, 4, NKEY], BF16)
                V = sb(ph, [128, NKC, 512], BF16)
                qt = [sb(ph, [128, 4, TS], BF16) for _ in range(2)]
                ut1 = sb(ph, [128, 4, TS + 16], F32)
                ut = [ut1, ut1]
                rc1 = sb(ph, [128, 4, TS], F32)
                rc = [rc1, rc1]
                hA = [sb(ph, [128, 8, TS], F32, 8) for _ in range(2)]
                E = [sb(ph, [128, TS], BF16) for _ in range(4)]
                mg = sb(ph, [128, 8, TS], BF16, 8)
                r1 = sb(ph, [128, TS], F32)
                r2 = sb(ph, [128, TS], F32)
                oa = sb(ph, [128, TS], F32)
                ob = sb(ph, [128, TS], F32)
                osq = sb(ph, [128, TS], BF16)
                pw = [sb(ph, [128, TS + 16], F32) for _ in range(2)]
                pb = [sb(ph, [128, TS], BF16) for _ in range(4)]
                xb = sb(ph, [128, 2, TS], BF16, 2)
                xsq = sb(ph, [128, 2, TS], BF16, 2)
                means = [sb(ph, [128, TS], F32) for _ in range(2)]
                rstds = [sb(ph, [128, TS], F32) for _ in range(2)]
                tail_q = []
                cs = 0
                ce = 0
                for b in range(NB):
                    load(KT.r, KT.t[:], KT_d[b], dsem_ld[4])
                    load(V.r, V.t[:], V_d[b], dsem_ld[5])
                    for tl in range(TPB):
                        t = b * TPB + tl
                        s = t % 2
                        t0 = tl * TS
                        h = hA[s]
                        load(qt[s].r, qt[s].t[:].rearrange("p h n -> p (h n)"), QT_d[t], dsem_ld[2 + s])
                        load(ut[s].r, ut[s].t[:], U_d[b][:, :, t0:t0 + TS + 16], dsem_ld[6])
                        load(rc[s].r, rc[s].t[:], rcnt_d[:, :, t0:t0 + TS], dsem_ld[10])
                        load(h.regs, h.t[:].rearrange("p c n -> p (c n)"), src_ha[t], dsem_ld[s])
                        def pool_dve(s=s):
                            for g, w in enumerate(POOL_W):
                                cur, cw, cr = ut[s].t[:, g, :], TS + 16, ut[s].r
                                step = 1
                                pi = 0
                                while step < w:
                                    nw = cw - step
                                    o_ = pw[pi % 2]
                                    dve(lambda e, o_=o_, cur=cur, nw=nw, step=step: e.tensor_tensor(
                                        out=o_.t[:, 0:nw], in0=cur[:, 0:nw], in1=cur[:, step:step + nw], op=ALU.add),
                                        [cr], [o_.r])
                                    cur, cw, cr = o_.t, nw, o_.r
                                    step *= 2
                                    pi += 1
                                o_ = pw[pi % 2]
                                off = 8 - w // 2
                                dve(lambda e, o_=o_, cur=cur, off=off, g=g, s=s: e.tensor_tensor(
                                    out=o_.t[:, 0:TS], in0=cur[:, off:off + TS], in1=rc[s].t[:, g, :], op=ALU.mult),
                                    [cr, rc[s].r], [o_.r])
                                pbt = pb[g]
                                dve(lambda e, o_=o_, pbt=pbt, g=g, s=s: e.tensor_tensor(
                                    out=pbt.t[:], in0=o_.t[:, 0:TS], in1=ut[s].t[:, g, 8:8 + TS], op=ALU.subtract),
                                    [o_.r, ut[s].r], [pbt.r])

                        def pool_mm():
                            nonlocal cs
                            for g in range(4):
                                R = PS[cs % 4]
                                cs += 1
                                pbt = pb[g]
                                mm(R, R.t[:], Wp.t[:, g, :], pbt.t[:], True, True, [Wp.r, pbt.r])
                                act(mg.regs[4 + g], mg.t[:, 4 + g, :], R.t[:], AF.Identity, [R.r, colsC.r],
                                    scale=colsC.t[:, g:g + 1])

                        O1, O2, Z1, Z2 = PS[4], PS[5], PS[6], PS[7]
                        if "noheads" in str(debug):
                            for c8 in range(8):
                                dve(lambda e, c8=c8: e.memset(mg.t[:, c8, :], 0.0), [], [mg.regs[c8]])
                        pending = []
                        pool_done = False
                        for hh in range(0 if "noheads" in str(debug) else 4):
                            def emit_S(kc, hh=hh, s=s):
                                nonlocal cs, ce
                                S1, S2 = PS[cs % 4], PS[(cs + 1) % 4]
                                cs += 2
                                ksl = slice(kc * 128, (kc + 1) * 128)
                                mm(S1, S1.t[:], KT.t[0:64, hh, ksl], qt[s].t[0:64, hh, :], True, True, [KT.r, qt[s].r])
                                mm(S2, S2.t[:], KT.t[64:128, hh, ksl], qt[s].t[64:128, hh, :], True, True,
                                   [KT.r, qt[s].r])
                                E1, E2 = E[ce % 4], E[(ce + 1) % 4]
                                ce += 2
                                act(E1.r, E1.t[:], S1.t[:], AF.Exp, [S1.r], scale=0.125)
                                act(E2.r, E2.t[:], S2.t[:], AF.Exp, [S2.r], scale=0.125)
                                return E1, E2

                            def emit_O(kc, E1, E2, hh=hh):
                                st_, sp_ = kc == 0, kc == NKC - 1
                                vsl = V.t[:, kc, hh * 128:(hh + 1) * 128]
                                mm(O1, O1.t[:], vsl, E1.t[:], st_, sp_, [V.r, E1.r])
                                mm(Z1, Z1.t[:], ones1.t[:], E1.t[:], st_, sp_, [ones1.r, E1.r])
                                mm(O2, O2.t[:], vsl, E2.t[:], st_, sp_, [V.r, E2.r])
                                mm(Z2, Z2.t[:], ones1.t[:], E2.t[:], st_, sp_, [ones1.r, E2.r])

                            prev = emit_S(0)
                            for kc in range(NKC):
                                nxt = emit_S(kc + 1) if kc + 1 < NKC else None
                                emit_O(kc, *prev)
                                prev = nxt
                                if pending and kc in (1, 3, 5):
                                    pending.pop(0)()
                                if hh == 1 and kc == 7:
                                    pool_dve()
                                    pool_done = True
                                if hh == 0 and tail_q and kc >= 8:
                                    tail_q.pop(0)()
                            while pending:
                                pending.pop(0)()
                            if hh == 0:
                                while tail_q:
                                    tail_q.pop(0)()
                            if hh == 1 and not pool_done:
                                pool_dve()
                            act(oa.r, oa.t[:], O1.t[:], AF.Copy, [O1.r])
                            dve(lambda e: e.tensor_copy(out=ob.t[:], in_=O2.t[:]), [O2.r], [ob.r])
                            act(r1.r, r1.t[:], Z1.t[:], AF.Ln, [Z1.r])
                            act(r2.r, r2.t[:], Z2.t[:], AF.Ln, [Z2.r])

                            def stB():
                                act(r1.r, r1.t[:], r1.t[:], AF.Exp, [r1.r], scale=-1.0)
                                act(r2.r, r2.t[:], r2.t[:], AF.Exp, [r2.r], scale=-1.0)
                                dve(lambda e: e.tensor_tensor(out=oa.t[:], in0=oa.t[:], in1=r1.t[:], op=ALU.mult),
                                    [oa.r, r1.r], [oa.r])
                                dve(lambda e: e.tensor_tensor(out=ob.t[:], in0=ob.t[:], in1=r2.t[:], op=ALU.mult),
                                    [ob.r, r2.r], [ob.r])
                                dve(lambda e: e.scalar_tensor_tensor(out=oa.t[:], in0=ob.t[:], scalar=lamc.t[:, 0:1],
                                                                     in1=oa.t[:], op0=ALU.mult, op1=ALU.add),
                                    [ob.r, oa.r, lamc.r], [oa.r])
                                act(osq.r, osq.t[:], oa.t[:], AF.Square, [oa.r])

                            def stC(Rb=None):
                                nonlocal cs
                                if Rb is None:
                                    R = PS[cs % 4]
                                    cs += 1
                                else:
                                    R = Rb
                                mm(R, R.t[:], ones128.t[:], osq.t[:], True, True, [ones128.r, osq.r])
                                dve(lambda e, R=R: e.tensor_scalar(out=r1.t[:], in0=R.t[:], scalar1=EPS, scalar2=None,
                                                                   op0=ALU.add), [R.r], [r1.r])
                                act(r1.r, r1.t[:], r1.t[:], AF.Ln, [r1.r])

                            def stD(hh=hh):
                                act(r1.r, r1.t[:], r1.t[:], AF.Exp, [r1.r], scale=-0.5)
                                dve(lambda e: e.tensor_tensor(out=oa.t[:], in0=oa.t[:], in1=r1.t[:], op=ALU.mult),
                                    [oa.r, r1.r], [oa.r])
                                act(mg.regs[hh], mg.t[:, hh, :], oa.t[:], AF.Identity, [oa.r, sublc.r],
                                    scale=sublc.t[:, 0:1])

                            pending = [stB, stC, stD]
                            if hh == 3:
                                ybanks = [PS[0], PS[1], PS[2], PS[3], PS[6]]
                                for d in range(5):
                                    py = ybanks[d]
                                    for j, k in enumerate((0, 1, 2, 4, 5, 6, 7)):
                                        mm(py, py.t[:], Wo.t[:, k, d * 128:(d + 1) * 128], mg.t[:, k, :], j == 0, False,
                                           [Wo.r, mg.regs[k]])
                                stB()
                                stC(PS[7])
                                stD()
                                pending = []
                            if hh == 2:
                                pool_mm()
                        if debug:
                            store(mg.regs, MG_d[t], mg.t[:].rearrange("p c n -> p (c n)"), dsem_st[9])
                        psS, psQ = PS[4], PS[5]
                        for d in range(8):
                            if d < 5:
                                py = ybanks[d]
                                mm(py, py.t[:], Wo.t[:, 3, d * 128:(d + 1) * 128], mg.t[:, 3, :], False, True,
                                   [Wo.r, mg.regs[3]])
                            else:
                                py = ybanks[d - 5]
                                for k in range(8):
                                    mm(py, py.t[:], Wo.t[:, k, d * 128:(d + 1) * 128], mg.t[:, k, :], k == 0, k == 7,
                                       [Wo.r, mg.regs[k]])
                            dve(lambda e, d=d, py=py, h=h, b=b: e.scalar_tensor_tensor(
                                out=h.t[:, d, :], in0=py.t[:], scalar=GAT.t[:, l, i, d, b:b + 1], in1=h.t[:, d, :],
                                op0=ALU.mult, op1=ALU.add), [py.r, h.regs[d], GAT.r], [h.regs[d]])
                            ln_stats(h.t[:, d, :], h.regs[d], d, xb, xsq, psS, psQ)
                        mean, rstd = means[s], rstds[s]
                        ln_finish(psS, psQ, mean, rstd)

                        def make_tail(t=t, h=h, mean=mean, rstd=rstd, s=s):
                            def chunk(d):
                                ln_norm(h.t[:, d, :], h.regs[d], mean, rstd)
                                act(h.regs[d], h.t[:, d, :], h.t[:, d, :], AF.Identity, [h.regs[d], LNGA.r, LNBA.r],
                                    bias=LNBA.t[:, li + d:li + d + 1], scale=LNGA.t[:, li + d:li + d + 1])
                                if d == 7:
                                    store(h.regs, dst_ha[t], h.t[:].rearrange("p c n -> p (c n)"), dsem_st[s])
                            return [lambda d=d: chunk(d) for d in range(8)]
                        tail_q = make_tail()
                while tail_q:
                    tail_q.pop(0)()
                P.barrier()

        def phase_conv_a(src_ha):
            l, i = 1, 1
            with ExitStack() as ph:
                stg = [sb(ph, [128, STG_N], F32) for _ in range(2)]
                W = sb(ph, [128, 8, 2048], BF16)
                load_w(stg, W, w_c1[0], D, 2048)
                hA = [sb(ph, [128, 8, TS], F32, 8) for _ in range(2)]
                xm = [sb(ph, [128, 8, TS], BF16, 8) for _ in range(2)]
                zo = [sb(ph, [128, 8, TS], BF16) for _ in range(2)]
                sg = [sb(ph, [128, TS], F32) for _ in range(2)]
                zt = sb(ph, [128, 8, 16], BF16)
                dve(lambda e: e.memset(zt.t[:], 0.0), [], [zt.r])
                for b in range(NB):
                    store(zt.r, ZC_d[b][:, :, 0:16], zt.t[:], dsem_st[8])
                    store(zt.r, ZC_d[b][:, :, TT + 16:TT + 32], zt.t[:], dsem_st[8])
                for t in range(NTL):
                    s = t % 2
                    b, tl = t // TPB, t % TPB
                    t0 = tl * TS
                    h = hA[s]
                    load(h.regs, h.t[:].rearrange("p c n -> p (c n)"), src_ha[t], dsem_ld[s])
                    for c in range(8):
                        act(xm[s].regs[c], xm[s].t[:, c, :], h.t[:, c, :], AF.Identity, [h.regs[c], AIN.r, MOD[l].r],
                            bias=MOD[l].t[:, 3 * i, c, b:b + 1], scale=AIN.t[:, l, i, c, b:b + 1])
                    for c in range(8):
                        pa, pg = PS[c % 2], PS[2 + c % 2]
                        for k in range(8):
                            mm(pa, pa.t[:], W.t[:, k, c * 128:(c + 1) * 128], xm[s].t[:, k, :], k == 0, k == 7,
                               [W.r, xm[s].regs[k]])
                        for k in range(8):
                            mm(pg, pg.t[:], W.t[:, k, D + c * 128:D + (c + 1) * 128], xm[s].t[:, k, :], k == 0, k == 7,
                               [W.r, xm[s].regs[k]])
                        sgt = sg[c % 2]
                        act(sgt.r, sgt.t[:], pg.t[:], AF.Sigmoid, [pg.r, colsC.r], bias=colsC.t[:, 12 + c:13 + c])
                        dve(lambda e, c=c, pa=pa, sgt=sgt, s=s: e.scalar_tensor_tensor(
                            out=zo[s].t[:, c, :], in0=pa.t[:], scalar=colsC.t[:, 4 + c:5 + c], in1=sgt.t[:],
                            op0=ALU.add, op1=ALU.mult), [pa.r, sgt.r, colsC.r], [zo[s].r])
                    store(zo[s].r, ZC_d[b][:, :, 16 + t0:16 + t0 + TS], zo[s].t[:], dsem_st[s])
                P.barrier()

        def phase_conv_b(src_ha, dst_ha):
            l, i = 1, 1
            li = (l * 3 + i) * 8
            with ExitStack() as ph:
                stg = [sb(ph, [128, STG_N], F32) for _ in range(2)]
                Wc2 = sb(ph, [128, 8, D], BF16)
                load_w(stg, Wc2, w_c2[0], D, D)
                DG = sb(ph, [128, CW * 8, 128], BF16)
                for idx in range(CW * 8):
                    col = colsD.t[:, idx:idx + 1] if idx < 124 else colsE.t[:, idx - 124:idx - 123]
                    dve(lambda e, idx=idx, col=col: e.tensor_scalar(out=DG.t[:, idx, :], in0=ident.t[:], scalar1=col,
                                                                    scalar2=None, op0=ALU.mult),
                        [ident.r, colsD.r, colsE.r], [DG.r])
                hA = [sb(ph, [128, 8, TS], F32, 8) for _ in range(2)]
                zt = [sb(ph, [128, 8, TS + 32], BF16) for _ in range(2)]
                cvs = [sb(ph, [128, 8, TS], F32, 8) for _ in range(2)]
                sc = sb(ph, [128, 8, TS], BF16, 8)
                xb = sb(ph, [128, 2, TS], BF16, 2)
                xsq = sb(ph, [128, 2, TS], BF16, 2)
                mean = sb(ph, [128, TS], F32)
                rstd = sb(ph, [128, TS], F32)
                mean2 = sb(ph, [128, TS], F32)
                rstd2 = sb(ph, [128, TS], F32)

                def issue_loads(t):
                    s = t % 2
                    b, tl = t // TPB, t % TPB
                    t0 = tl * TS
                    load(zt[s].r, zt[s].t[:], ZC_d[b][:, :, t0:t0 + TS + 32], dsem_ld[2 + s])
                    load(hA[s].regs, hA[s].t[:].rearrange("p c n -> p (c n)"), src_ha[t], dsem_ld[s])

                def conv(t):
                    s = t % 2
                    cv = cvs[s]
                    for c in range(8):
                        pc = PS[c % 2]
                        for j in range(CW):
                            mm(pc, pc.t[:], DG.t[:, j * 8 + c, :], zt[s].t[:, c, 1 + j:1 + j + TS], j == 0, j == CW - 1,
                               [DG.r, zt[s].r])
                        act(cv.regs[c], cv.t[:, c, :], pc.t[:], AF.Identity, [pc.r, colsC.r],
                            bias=colsC.t[:, 20 + c:21 + c])
                        ln_stats(cv.t[:, c, :], cv.regs[c], c, xb, xsq, PS[4], PS[5])

                def ln1chain(t):
                    cv = cvs[t % 2]
                    ln_finish(PS[4], PS[5], mean, rstd)
                    for c in range(8):
                        ln_norm(cv.t[:, c, :], cv.regs[c], mean, rstd)
                        act(sc.regs[c], sc.t[:, c, :], cv.t[:, c, :], AF.Silu, [cv.regs[c], colsC.r],
                            bias=colsC.t[:, 36 + c:37 + c], scale=colsC.t[:, 28 + c:29 + c])

                def c2(t):
                    s = t % 2
                    b = t // TPB
                    h = hA[s]
                    for d in range(8):
                        py = PS[2 + d % 2]
                        for c in range(8):
                            mm(py, py.t[:], Wc2.t[:, c, d * 128:(d + 1) * 128], sc.t[:, c, :], c == 0, c == 7,
                               [Wc2.r, sc.regs[c]])
                        dve(lambda e, d=d, h=h, b=b: e.tensor_scalar(out=h.t[:, d, :], in0=h.t[:, d, :],
                                                                     scalar1=bc2g.t[:, d, b:b + 1], scalar2=None,
                                                                     op0=ALU.add), [h.regs[d], bc2g.r], [h.regs[d]])
                        dve(lambda e, d=d, py=py, h=h, b=b: e.scalar_tensor_tensor(
                            out=h.t[:, d, :], in0=py.t[:], scalar=GAT.t[:, l, i, d, b:b + 1], in1=h.t[:, d, :],
                            op0=ALU.mult, op1=ALU.add), [py.r, h.regs[d], GAT.r], [h.regs[d]])
                        ln_stats(h.t[:, d, :], h.regs[d], d, xb, xsq, PS[6], PS[7])
                    ln_finish(PS[6], PS[7], mean2, rstd2)
                    for d in range(8):
                        ln_norm(h.t[:, d, :], h.regs[d], mean2, rstd2)
                        act(h.regs[d], h.t[:, d, :], h.t[:, d, :], AF.Identity, [h.regs[d], LNGA.r, LNBA.r],
                            bias=LNBA.t[:, li + d:li + d + 1], scale=LNGA.t[:, li + d:li + d + 1])
                    store(h.regs, dst_ha[t], h.t[:].rearrange("p c n -> p (c n)"), dsem_st[s])

                issue_loads(0)
                conv(0)
                for t in range(NTL):
                    ln1chain(t)
                    if t + 1 < NTL:
                        issue_loads(t + 1)
                        conv(t + 1)
                    c2(t)
                P.barrier()

        phase_ffn_a(0, 0, None, HA[0], tiles=list(range(NTILE)))
        phase_ffn_b(0, 0, HA[0], HA[1], tiles=list(range(NTILE)))
        if debug == "ffn":
            phase_ffn_a(1, 2, HA[1])
            phase_ffn_b(1, 2, HA[1], None, final=True)
        else:
            phase_mix_a(HA[1])
            if debug != "mixa":
                phase_attn(HA[1], HA[0])
            if not debug or debug == "full":
                phase_ffn_a(0, 2, HA[0])
                phase_ffn_b(0, 2, HA[0], HA[1])
                phase_ffn_a(1, 0, HA[1])
                phase_ffn_b(1, 0, HA[1], HA[0])
                phase_conv_a(HA[0])
                phase_conv_b(HA[0], HA[1])
                phase_ffn_a(1, 2, HA[1])
                phase_ffn_b(1, 2, HA[1], None, final=True)

        P.barrier()

        with nc.Block() as block:
            @block.tensor
            def _(e):
                P.replay("pe", e)

            @block.scalar
            def _(e):
                P.replay("act", e)

            @block.vector
            def _(e):
                P.replay("dve", e)

            @block.gpsimd
            def _(e):
                P.replay("pool", e)

            @block.sync
            def _(e):
                P.replay("sp", e)
    return nc


def host_consts(TT):
    ident = np.eye(128, dtype=np.float32)
    p = np.arange(128)
    r = p % 64
    axis = r // 32
    half = (r % 32) // 16
    f = r % 16
    inv_freq = (np.float32(10000.0) ** (-np.arange(16, dtype=np.float32) / np.float32(16))).astype(np.float32)
    t = np.arange(TT)
    row = (t // 64).astype(np.float32)
    col = (t % 64).astype(np.float32)
    pos = np.where(axis[:, None] == 0, row[None, :], col[None, :]).astype(np.float32)
    ang = (pos * inv_freq[f][:, None]).astype(np.float32)
    cos = np.cos(ang).astype(np.float32)
    sin = np.sin(ang).astype(np.float32)
    sins = np.where(half[:, None] == 0, -sin, sin).astype(np.float32)
    rc = np.zeros((4, TT), np.float32)
    for g, w in enumerate(POOL_W):
        lo = np.clip(t - w // 2, 0, TT)
        hi = np.clip(t - w // 2 + w, 0, TT)
        rc[g] = (1.0 / (hi - lo)).astype(np.float32)
    rcnt = np.ascontiguousarray(np.broadcast_to(rc[None], (128, 4, TT))).astype(np.float32)
    return {"ident": ident, "rope_cos": cos, "rope_sin": sins, "rcnt": rcnt}


_W_KEYS = ("w_ada", "b_ada", "ln_g", "ln_b", "w_ffn_in", "w_ffn_out", "w_mix_in", "w_mix_out", "lam_q1",
           "lam_k1", "lam_q2", "lam_k2", "subln_g", "w_pool", "pool_scale", "w_c1", "b_c1", "w_dw", "b_dw",
           "conv_ln_g", "conv_ln_b", "w_c2", "b_c2")


def make_in_maps(inputs, NB, TT, ncores):
    hc = host_consts(TT)
    maps = []
    for i in range(ncores):
        m = {k: np.ascontiguousarray(np.asarray(inputs[k], dtype=np.float32)) for k in _W_KEYS}
        xb = np.asarray(inputs["x"])[i * NB:(i + 1) * NB]
        m["x"] = np.ascontiguousarray(xb.reshape(NB * TT, D))
        m["ctx"] = np.ascontiguousarray(np.asarray(inputs["ctx"])[i * NB:(i + 1) * NB].reshape(NB * CTX, D))
        m["cc"] = np.ascontiguousarray(np.concatenate(
            [np.asarray(inputs["c"])[i * NB:(i + 1) * NB], np.asarray(inputs["c_ctx"])[None, :]], axis=0))
        m.update(hc)
        maps.append(m)
    return maps


def kernel(**inputs):
    B, TT, _ = inputs["x"].shape
    ncores = 8
    NB = B // ncores
    nc = build(NB, TT)
    maps = make_in_maps(inputs, NB, TT, ncores)
    res = run_bass_kernel_spmd(nc, maps, core_ids=list(range(ncores)))
    outs = [np.asarray(r["out"]).reshape(NB, TT, D) for r in res.results]
    return np.concatenate(outs, axis=0).astype(np.float32)
```

```python
import math
from contextlib import ExitStack
import numpy as np
import concourse.bass as bass
import concourse.mybir as mybir
from concourse.bass_utils import run_bass_kernel_spmd

F32 = mybir.dt.float32
BF16 = mybir.dt.bfloat16
AF = mybir.ActivationFunctionType
ALU = mybir.AluOpType

D = 1024
DFF = 2816
NFC = 22
DEPTH = 2
ALPHA = (2.0 * DEPTH) ** 0.25
EPS = 1e-5
CTX = 256
TS = 512
POOL_W = (2, 4, 8, 16)
CW = 31
LAM_INIT0 = 0.8 - 0.6 * math.exp(-0.3 * 0)


class Sem:
    def __init__(self, h):
        self.h = h
        self.n = 0


class Reg:
    __slots__ = ("w", "r")

    def __init__(self):
        self.w = None
        self.r = {}


class Prog:
    ENG = ("pe", "act", "dve", "pool", "sp")

    def __init__(self, nc, sem_handles):
        self.nc = nc
        self.free = list(sem_handles)
        self.q = {e: [] for e in self.ENG}
        self.waited = {e: {} for e in self.ENG}
        self.pend = {e: [] for e in self.ENG}
        self.esem = {}
        self.allsems = []
        for e in ("pe", "act", "dve"):
            self.esem[e] = self.new_sem()

    def new_sem(self):
        s = Sem(self.free.pop())
        self.allsems.append(s)
        return s

    def op(self, eng, fn, reads=(), writes=(), dsem=None, signal=True):
        wt = self.waited[eng]
        waits = {}
        own = dsem if dsem is not None else self.esem[eng]

        def need(tok):
            if tok is None:
                return
            s, v = tok
            if dsem is not None and s is dsem:
                return
            if eng == "pe" and s is own:
                return
            if wt.get(s, 0) >= v:
                return
            if waits.get(s, 0) < v:
                waits[s] = v

        for r in reads:
            need(r.w)
        for w in writes:
            need(w.w)
            for s, v in w.r.items():
                need((s, v))
        for s, v in waits.items():
            wt[s] = v
        wl = list(waits.items())
        if not signal:
            self.pend[eng].extend(reads)
            self.q[eng].append((fn, wl, None, 0))
            return None
        inc = 16 if dsem is not None else 1
        own.n += inc
        tok = (own, own.n)
        for r in list(reads) + self.pend[eng]:
            if r.r.get(own, 0) < own.n:
                r.r[own] = own.n
        self.pend[eng] = []
        for w in writes:
            w.w = tok
            w.r = {}
        self.q[eng].append((fn, wl, own, inc))
        return tok

    def barrier(self):
        for e in self.ENG:
            assert not self.pend[e]
            for s in self.allsems:
                if s.n > 0 and self.waited[e].get(s, 0) < s.n:
                    self.q[e].append((None, [(s, s.n)], None, 0))
                    self.waited[e][s] = s.n
        for e in ("pe", "act", "dve"):
            self.esem[e] = self.new_sem()

    def replay(self, eng, h):
        for fn, waits, sem, inc in self.q[eng]:
            for s, v in waits:
                h.wait_ge(s.h, v)
            if fn is None:
                continue
            ins = fn(h)
            if sem is not None:
                ins.then_inc(sem.h, inc)


class T:
    def __init__(self, t, nreg=1):
        self.t = t
        self.regs = [Reg() for _ in range(nreg)]

    @property
    def r(self):
        return self.regs[0]


def build(NB, TT, debug=False):
    NTL = NB * TT // TS
    TPB = TT // TS
    NTILE = NTL + 1
    NKC = (CTX + TT) // 128
    NKEY = CTX + TT
    nc = bass.Bass("TRN2", target_bir_lowering=False)

    def din(name, shape, dt=F32):
        return nc.dram_tensor(name, list(shape), dt, kind="ExternalInput").ap()

    x_d = din("x", [NB * TT, D])
    ctx_d = din("ctx", [NB * CTX, D])
    cc_d = din("cc", [NB + 1, D])
    w_ada = din("w_ada", [2, D, 9 * D])
    b_ada = din("b_ada", [2, 9 * D])
    ln_g = din("ln_g", [2, 3, D])
    ln_b = din("ln_b", [2, 3, D])
    w_ffn_in = din("w_ffn_in", [2, 2, D, 2 * DFF])
    w_ffn_out = din("w_ffn_out", [2, 2, DFF, D])
    w_mix_in = din("w_mix_in", [1, D, 2048])
    w_mix_out = din("w_mix_out", [1, D, D])
    lam_d = {k: din(k, [1, 64]) for k in ("lam_q1", "lam_k1", "lam_q2", "lam_k2")}
    subln_g = din("subln_g", [1, 128])
    w_pool = din("w_pool", [1, 4, 128, 128])
    pool_scale = din("pool_scale", [1, 512])
    w_c1 = din("w_c1", [1, D, 2 * D])
    b_c1 = din("b_c1", [1, 2 * D])
    w_dw = din("w_dw", [1, CW, D])
    b_dw = din("b_dw", [1, D])
    conv_ln_g = din("conv_ln_g", [1, D])
    conv_ln_b = din("conv_ln_b", [1, D])
    w_c2 = din("w_c2", [1, D, D])
    b_c2 = din("b_c2", [1, D])
    ident_d = din("ident", [128, 128])
    cos_d = din("rope_cos", [128, TT])
    sin_d = din("rope_sin", [128, TT])
    rcnt_d = din("rcnt", [128, 4, TT])
    out_d = nc.dram_tensor("out", [NB * TT, D], F32, kind="ExternalOutput").ap()

    def dscr(name, shape, dt):
        kind = "ExternalOutput" if debug else "Internal"
        return nc.dram_tensor(name, list(shape), dt, kind=kind).ap()

    HA = [dscr(f"ha{i}", [NTILE, 128, 8 * TS], F32) for i in range(2)]
    HMID = dscr("hmid", [NTILE, 128, NFC * TS], BF16)
    QT_d = dscr("qt", [NTL, 128, 4 * TS], BF16)
    KT_d = dscr("kt", [NB, 128, 4, NKEY], BF16)
    V_d = dscr("v", [NB, 128, NKC, 512], BF16)
    U_d = dscr("u", [NB, 128, 4, TT + 16], F32)
    ZC_d = dscr("zc", [NB, 128, 8, TT + 32], BF16)
    MG_d = dscr("mgd", [NTL, 128, 8 * TS], BF16) if debug else None

    es = ExitStack()
    with es:
        nsem = 100
        sems = [es.enter_context(nc.semaphore(f"s{i}")) for i in range(nsem)]
        P = Prog(nc, sems)
        cnt = [0]

        def sb(stack, shape, dt, nreg=1):
            cnt[0] += 1
            return T(stack.enter_context(nc.sbuf_tensor(f"sb{cnt[0]}", list(shape), dt)), nreg)

        PS = [T(es.enter_context(nc.psum_tensor(f"ps{i}", [128, 512], F32))) for i in range(8)]

        ident = sb(es, [128, 128], F32)
        onesb = sb(es, [128, 128], BF16)
        ones128 = sb(es, [128, 128], BF16)
        ones1 = sb(es, [128, 128], BF16)
        MOD = [sb(es, [128, 9, 8, 3], F32) for _ in range(2)]
        colsA = sb(es, [128, 120], F32)
        colsB = sb(es, [128, 120], F32)
        colsC = sb(es, [128, 64], F32)
        colsD = sb(es, [128, 124], F32)
        colsE = sb(es, [128, 124], F32)
        AIN = sb(es, [128, 2, 3, 8, 3], F32)
        GAT = sb(es, [128, 2, 3, 8, 3], F32)
        LNGA = sb(es, [128, 48], F32)
        LNBA = sb(es, [128, 48], F32)
        lamc = sb(es, [128, 4], F32)
        sublc = sb(es, [128, 1], F32)
        bc2g = sb(es, [128, 8, 3], F32)
        dsem_ld = [P.new_sem() for _ in range(18)]
        dsem_st = [P.new_sem() for _ in range(12)]

        def mm(out, out_ap, lhsT, rhs, start, stop, reads):
            P.op("pe", lambda e: e.matmul(out_ap, lhsT=lhsT, rhs=rhs, start=start, stop=stop),
                 reads=reads, writes=[out.r], signal=stop)

        def act(out_reg, out_ap, in_ap, func, reads, bias=None, scale=None):
            kw = {}
            if bias is not None:
                kw["bias"] = bias
            if scale is not None:
                kw["scale"] = scale
            return P.op("act", lambda e: e.activation(out=out_ap, in_=in_ap, func=func, **kw),
                        reads=reads, writes=[out_reg])

        def dve(fn, reads, writes):
            return P.op("dve", fn, reads=reads, writes=writes)

        def load(dst_reg, dst_ap, src_ap, sem, reads=()):
            wl = list(dst_reg) if isinstance(dst_reg, (list, tuple)) else [dst_reg]
            return P.op("sp", lambda e: e.dma_start(out=dst_ap, in_=src_ap), reads=reads,
                        writes=wl, dsem=sem)

        def store(src_reg, dst_ap, src_ap, sem):
            rl = list(src_reg) if isinstance(src_reg, (list, tuple)) else [src_reg]
            return P.op("pool", lambda e: e.dma_start(out=dst_ap, in_=src_ap), reads=rl,
                        writes=[], dsem=sem)

        castrr = [0]

        def cast(out_reg, out_ap, in_ap, reads):
            castrr[0] += 1
            if castrr[0] % 2:
                act(out_reg, out_ap, in_ap, AF.Copy, reads)
            else:
                dve(lambda e: e.tensor_copy(out=out_ap, in_=in_ap), reads, [out_reg])

        STG_N = 2816

        def load_w(stack_stg, dst, src, K, N, col0=0, ncols=None, sems=None, stg_n=STG_N):
            ncols = ncols or N
            KC = K // 128
            cb = min(ncols, stg_n)
            per = max(1, min(KC, stg_n // cb))
            sems = sems or dsem_ld[8:10]
            i = 0
            for c0 in range(0, ncols, cb):
                c1 = min(ncols, c0 + cb)
                for k0 in range(0, KC, per):
                    k1 = min(KC, k0 + per)
                    st = stack_stg[i % 2]
                    i += 1
                    sap = st.t[:, 0:(k1 - k0) * (c1 - c0)].rearrange("p (c n) -> p c n", n=c1 - c0)
                    srcap = src[k0 * 128:k1 * 128, col0 + c0:col0 + c1].rearrange("(c p) n -> p c n", p=128)
                    load(st.r, sap, srcap, sems[i % 2])
                    cast(dst.r, dst.t[:, k0:k1, c0:c1], sap, [st.r])

        def load_w_blocks(stack_stg, dst, src, K, blocks):
            KC = K // 128
            wsems = [dsem_ld[8], dsem_ld[9], dsem_ld[14], dsem_ld[15]]
            slots = []
            for st in stack_stg:
                slots.append((st, 0))
            width = STG_N
            need = max(KC * nc_ for blk in blocks for _, nc_ in blk)
            if need * 2 <= STG_N:
                slots = [(st, o) for st in stack_stg for o in (0, STG_N // 2)]
                width = STG_N // 2
            regs = {}
            for st, o in slots:
                regs[(id(st), o)] = Reg()
            i = 0
            for r, blk in enumerate(blocks):
                for c0, nc_ in blk:
                    st, o = slots[i % len(slots)]
                    rg = regs[(id(st), o)]
                    sem = wsems[i % len(slots)]
                    i += 1
                    sap = st.t[:, o:o + KC * nc_].rearrange("p (c n) -> p c n", n=nc_)
                    srcap = src[:, c0:c0 + nc_].rearrange("(c p) n -> p c n", p=128)
                    load(rg, sap, srcap, sem)
                    cast(dst.regs[r], dst.t[:, :, c0:c0 + nc_], sap, [rg])

        with ExitStack() as ph:
            stg = [sb(ph, [128, STG_N], F32) for _ in range(2)]
            rows = [sb(ph, [128, 128], F32) for _ in range(5)]
            ccs = sb(ph, [4, D], F32)
            scT = sb(ph, [128, 8, 4], BF16)
            wblk = [sb(ph, [128, 8, 1024], BF16) for _ in range(2)]
            lamt = sb(ph, [128, 4, 64], F32)
            load(ident.r, ident.t[:], ident_d, dsem_ld[0])
            dve(lambda e: e.memset(onesb.t[:], 1.0 / 1024), [], [onesb.r])
            dve(lambda e: e.memset(ones128.t[:], 1.0 / 128), [], [ones128.r])
            dve(lambda e: e.memset(ones1.t[:], 1.0), [], [ones1.r])
            for r_ in rows:
                dve(lambda e, r_=r_: e.memset(r_.t[:], 0.0), [], [r_.r])

            def vrow(rt, r0, vec_ap, n):
                load(rt.r, rt.t[r0:r0 + n, :], vec_ap, dsem_ld[1])

            vrow(rows[0], 0, b_ada[0].rearrange("(c p) -> c p", p=128), 72)
            vrow(rows[0], 72, ln_g.rearrange("l i (c p) -> (l i c) p", p=128), 48)
            vrow(rows[1], 0, b_ada[1].rearrange("(c p) -> c p", p=128), 72)
            vrow(rows[1], 72, ln_b.rearrange("l i (c p) -> (l i c) p", p=128), 48)
            o = 0
            for v_ap, n in ((pool_scale[0], 4), (b_c1[0], 16), (b_dw[0], 8), (conv_ln_g[0], 8),
                            (conv_ln_b[0], 8), (b_c2[0], 8), (subln_g[0], 1)):
                vrow(rows[2], o, v_ap.rearrange("(c p) -> c p", p=128), n)
                o += n
            wdw_rows = w_dw[0].rearrange("j (c p) -> (j c) p", p=128)
            vrow(rows[3], 0, wdw_rows[0:124, :], 124)
            vrow(rows[4], 0, wdw_rows[124:248, :], 124)
            for rt in rows:
                rt.r.w = (dsem_ld[1], dsem_ld[1].n)
            for rt, ct, n in ((rows[0], colsA, 120), (rows[1], colsB, 120), (rows[2], colsC, 64),
                              (rows[3], colsD, 124), (rows[4], colsE, 124)):
                P.op("pe", lambda e, rt=rt, n=n: e.transpose(out=PS[0].t[:, 0:n], in_=rt.t[0:n, :],
                                                               identity=ident.t[0:n, 0:n]),
                     reads=[rt.r, ident.r], writes=[PS[0].r])
                dve(lambda e, ct=ct, n=n: e.tensor_copy(out=ct.t[:, 0:n], in_=PS[0].t[:, 0:n]),
                    [PS[0].r], [ct.r])
            NR = NB + 1
            load(ccs.r, ccs.t[0:NR, :], cc_d, dsem_ld[2])
            act(ccs.r, ccs.t[0:NR, :], ccs.t[0:NR, :], AF.Silu, [ccs.r])
            for k in range(8):
                P.op("pe", lambda e, k=k: e.transpose(out=PS[1].t[:, k * 4:k * 4 + NR],
                                                       in_=ccs.t[0:NR, k * 128:(k + 1) * 128],
                                                       identity=ident.t[0:NR, 0:NR]),
                     reads=[ccs.r, ident.r], writes=[PS[1].r])
            dve(lambda e: e.memset(scT.t[:], 0.0), [], [scT.r])
            dve(lambda e: e.tensor_copy(out=scT.t[:, :, 0:NR],
                                        in_=PS[1].t[:, 0:32].rearrange("p (k r) -> p k r", r=4)[:, :, 0:NR]),
                [PS[1].r], [scT.r])
            for l in range(2):
                bcols = colsA if l == 0 else colsB
                for jb in range(9):
                    wb = wblk[jb % 2]
                    load_w(stg, wb, w_ada[l], D, 9 * D, col0=jb * 1024, ncols=1024)
                    pst = PS[2 + jb % 2]
                    for oc in range(8):
                        for k in range(8):
                            mm(pst, pst.t[:, oc * 4:oc * 4 + NR], wb.t[:, k, oc * 128:(oc + 1) * 128],
                               scT.t[:, k, 0:NR], k == 0, k == 7, [wb.r, scT.r])
                    for r_ in range(NR):
                        dve(lambda e, l=l, jb=jb, r_=r_, pst=pst, bcols=bcols: e.tensor_tensor(
                            out=MOD[l].t[:, jb, :, r_],
                            in0=pst.t[:, 0:32].rearrange("p (c r) -> p c r", r=4)[:, :, r_],
                            in1=bcols.t[:, jb * 8:(jb + 1) * 8], op=ALU.add),
                            [pst.r, bcols.r], [MOD[l].r])
            for l in range(2):
                for i in range(3):
                    dve(lambda e, l=l, i=i: e.tensor_scalar(out=AIN.t[:, l, i], in0=MOD[l].t[:, 3 * i + 1],
                                                            scalar1=1.0, scalar2=1.0 / ALPHA,
                                                            op0=ALU.add, op1=ALU.mult),
                        [MOD[l].r], [AIN.r])
                    rw = 1.0 if i == 1 else 0.5
                    dve(lambda e, l=l, i=i, rw=rw: e.tensor_scalar(out=GAT.t[:, l, i], in0=MOD[l].t[:, 3 * i + 2],
                                                                   scalar1=rw, scalar2=None, op0=ALU.mult),
                        [MOD[l].r], [GAT.r])
            dve(lambda e: e.tensor_scalar(out=LNGA.t[:], in0=colsA.t[:, 72:120], scalar1=ALPHA, scalar2=None,
                                          op0=ALU.mult), [colsA.r], [LNGA.r])
            dve(lambda e: e.tensor_scalar(out=LNBA.t[:], in0=colsB.t[:, 72:120], scalar1=ALPHA, scalar2=None,
                                          op0=ALU.mult), [colsB.r], [LNBA.r])
            for r_ in range(NR):
                dve(lambda e, r_=r_: e.tensor_tensor(out=bc2g.t[:, :, r_], in0=GAT.t[:, 1, 1, :, r_],
                                                     in1=colsC.t[:, 44:52], op=ALU.mult),
                    [GAT.r, colsC.r], [bc2g.r])
            dve(lambda e: e.tensor_scalar(out=sublc.t[:], in0=colsC.t[:, 52:53], scalar1=1.0 - LAM_INIT0,
                                          scalar2=None, op0=ALU.mult), [colsC.r], [sublc.r])
            for j, k in enumerate(("lam_q1", "lam_k1", "lam_q2", "lam_k2")):
                load(lamt.r, lamt.t[:, j, :], lam_d[k].partition_broadcast(128), dsem_ld[3])
            dve(lambda e: e.tensor_tensor(out=lamt.t[:, 0, :], in0=lamt.t[:, 0, :], in1=lamt.t[:, 1, :],
                                          op=ALU.mult), [lamt.r], [lamt.r])
            dve(lambda e: e.tensor_tensor(out=lamt.t[:, 2, :], in0=lamt.t[:, 2, :], in1=lamt.t[:, 3, :],
                                          op=ALU.mult), [lamt.r], [lamt.r])
            dve(lambda e: e.reduce_sum(out=lamc.t[:, 1:2], in_=lamt.t[:, 0, :], axis=mybir.AxisListType.X),
                [lamt.r], [lamc.r])
            dve(lambda e: e.reduce_sum(out=lamc.t[:, 2:3], in_=lamt.t[:, 2, :], axis=mybir.AxisListType.X),
                [lamt.r], [lamc.r])
            act(lamc.r, lamc.t[:, 1:3], lamc.t[:, 1:3], AF.Exp, [lamc.r])
            dve(lambda e: e.scalar_tensor_tensor(out=lamc.t[:, 0:1], in0=lamc.t[:, 2:3], scalar=-LAM_INIT0,
                                                 in1=lamc.t[:, 1:2], op0=ALU.add, op1=ALU.subtract),
                [lamc.r], [lamc.r])
            P.barrier()

        def mod_row(t):
            return NB if t == NTL else t // TPB

        def ln_cast(xap, xreg, c, xb, xsq):
            sl = c % 2
            act(xb.regs[sl], xb.t[:, sl, :], xap, AF.Copy, [xreg])
            act(xsq.regs[sl], xsq.t[:, sl, :], xap, AF.Square, [xreg])

        def ln_mm(c, xb, xsq, psS, psQ):
            sl = c % 2
            mm(psS, psS.t[:], onesb.t[:], xb.t[:, sl, :], c == 0, c == 7, [onesb.r, xb.regs[sl]])
            mm(psQ, psQ.t[:], onesb.t[:], xsq.t[:, sl, :], c == 0, c == 7, [onesb.r, xsq.regs[sl]])

        def ln_stats(xap, xreg, c, xb, xsq, psS, psQ):
            ln_cast(xap, xreg, c, xb, xsq)
            if c > 0:
                ln_mm(c - 1, xb, xsq, psS, psQ)
            if c == 7:
                ln_mm(7, xb, xsq, psS, psQ)

        def ln_finish(psS, psQ, mean, rstd):
            act(mean.r, mean.t[:], psS.t[:], AF.Copy, [psS.r])
            act(rstd.r, rstd.t[:], psS.t[:], AF.Square, [psS.r])
            dve(lambda e: e.scalar_tensor_tensor(out=rstd.t[:], in0=psQ.t[:], scalar=EPS, in1=rstd.t[:],
                                                 op0=ALU.add, op1=ALU.subtract), [psQ.r, rstd.r], [rstd.r])
            act(rstd.r, rstd.t[:], rstd.t[:], AF.Ln, [rstd.r])
            act(rstd.r, rstd.t[:], rstd.t[:], AF.Exp, [rstd.r], scale=-0.5)

        def ln_norm(xap, xreg, mean, rstd):
            dve(lambda e: e.tensor_tensor(out=xap, in0=xap, in1=mean.t[:], op=ALU.subtract),
                [xreg, mean.r], [xreg])
            dve(lambda e: e.tensor_tensor(out=xap, in0=xap, in1=rstd.t[:], op=ALU.mult),
                [xreg, rstd.r], [xreg])

        def tile_src(t):
            return (x_d[t * TS:(t + 1) * TS, :] if t < NTL else ctx_d[:, :]).rearrange("(s p) d -> p s d", p=128)

        def phase_ffn_a(l, i, src_ha, dst_ha=None, tiles=None):
            tiles = list(range(NTL)) if tiles is None else tiles
            with ExitStack() as ph:
                stg = [sb(ph, [128, STG_N], F32) for _ in range(2)]
                W = sb(ph, [128, 8, 2 * DFF], BF16, NFC)
                hA = [sb(ph, [128, 8, TS], F32, 8) for _ in range(2)]
                xin = [sb(ph, [128, 4, D], F32) for _ in range(1)] if src_ha is None else None
                xm = [sb(ph, [128, 8, TS], BF16, 8) for _ in range(2)]
                sg = [sb(ph, [128, TS], F32) for _ in range(2)]
                hm = [sb(ph, [128, 2, TS], BF16) for _ in range(3)]
                hmi = 0

                def prep_load(ti):
                    if src_ha is None or ti >= len(tiles):
                        return
                    t = tiles[ti]
                    h = hA[ti % 2]
                    load(h.regs, h.t[:].rearrange("p c n -> p (c n)"), src_ha[t], dsem_ld[ti % 2])

                def prep(ti):
                    t = tiles[ti]
                    s = ti % 2
                    r_ = mod_row(t)
                    h = hA[s]
                    if src_ha is None:
                        xi = xin[0]
                        load(xi.r, xi.t[:], tile_src(t), dsem_ld[0])
                        for c in range(8):
                            pst = PS[4 + c % 2]
                            for s4 in range(4):
                                P.op("pe", lambda e, c=c, s4=s4, pst=pst, xi=xi: e.transpose(
                                    out=pst.t[:, s4 * 128:(s4 + 1) * 128], in_=xi.t[:, s4, c * 128:(c + 1) * 128],
                                    identity=ident.t[:]), reads=[xi.r, ident.r], writes=[pst.r])
                            act(h.regs[c], h.t[:, c, :], pst.t[:], AF.Copy, [pst.r], scale=ALPHA)
                        store(h.regs, dst_ha[t], h.t[:].rearrange("p c n -> p (c n)"), dsem_st[s])
                    for c in range(8):
                        act(xm[s].regs[c], xm[s].t[:, c, :], h.t[:, c, :], AF.Identity, [h.regs[c], AIN.r, MOD[l].r],
                            bias=MOD[l].t[:, 3 * i, c, r_:r_ + 1], scale=AIN.t[:, l, i, c, r_:r_ + 1])

                prep_load(0)
                prep_load(1)
                prep(0)
                load_w_blocks(stg, W, w_ffn_in[l, i // 2], D,
                              [[(f * 128, 128), (DFF + f * 128, 128)] for f in range(NFC)])
                for ti, t in enumerate(tiles):
                    s = ti % 2
                    for f in range(NFC):
                        pg, pu = PS[f % 2], PS[2 + f % 2]
                        for k in range(8):
                            mm(pg, pg.t[:], W.t[:, k, f * 128:(f + 1) * 128], xm[s].t[:, k, :], k == 0, k == 7,
                               [W.regs[f], xm[s].regs[k]])
                        for k in range(8):
                            mm(pu, pu.t[:], W.t[:, k, DFF + f * 128:DFF + (f + 1) * 128], xm[s].t[:, k, :],
                               k == 0, k == 7, [W.regs[f], xm[s].regs[k]])
                        sgt = sg[f % 2]
                        act(sgt.r, sgt.t[:], pg.t[:], AF.Silu, [pg.r])
                        if f == 10 and ti + 1 < len(tiles):
                            prep(ti + 1)
                            prep_load(ti + 2)
                        hmt = hm[hmi % 3]
                        dve(lambda e, hmt=hmt, f=f, sgt=sgt, pu=pu: e.tensor_tensor(
                            out=hmt.t[:, f % 2, :], in0=sgt.t[:], in1=pu.t[:], op=ALU.mult),
                            [sgt.r, pu.r], [hmt.r])
                        if f % 2 == 1:
                            store(hmt.r, HMID[t][:, (f - 1) * TS:(f + 1) * TS],
                                  hmt.t[:].rearrange("p c n -> p (c n)"), dsem_st[2 + hmi % 3])
                            hmi += 1
                P.barrier()

        def phase_ffn_b(l, i, src_ha, dst_ha, tiles=None, final=False):
            tiles = list(range(NTL)) if tiles is None else tiles
            li = (l * 3 + i) * 8
            with ExitStack() as ph:
                stg = [sb(ph, [128, STG_N], F32) for _ in range(2)]
                W2 = sb(ph, [128, NFC, D], BF16, 8)
                hA = [sb(ph, [128, 8, TS], F32, 8) for _ in range(3)]
                hmd = [sb(ph, [128, NFC, TS], BF16) for _ in range(2)]
                xb = sb(ph, [128, 2, TS], BF16, 2)
                xsq = sb(ph, [128, 2, TS], BF16, 2)
                means = [sb(ph, [128, TS], F32) for _ in range(2)]
                rstds = [sb(ph, [128, TS], F32) for _ in range(2)]
                otm = sb(ph, [128, 4, D], F32) if final else None
                tail_q = []

                def make_tail(t, h, mean, rstd, s3):
                    def chunk(d):
                        ln_norm(h.t[:, d, :], h.regs[d], mean, rstd)
                        if final:
                            act(h.regs[d], h.t[:, d, :], h.t[:, d, :], AF.Identity, [h.regs[d], colsA.r, colsB.r],
                                bias=colsB.t[:, 72 + li + d:72 + li + d + 1],
                                scale=colsA.t[:, 72 + li + d:72 + li + d + 1])
                        else:
                            act(h.regs[d], h.t[:, d, :], h.t[:, d, :], AF.Identity, [h.regs[d], LNGA.r, LNBA.r],
                                bias=LNBA.t[:, li + d:li + d + 1], scale=LNGA.t[:, li + d:li + d + 1])
                        if d < 7:
                            return
                        if final:
                            for s4 in range(4):
                                for half in range(2):
                                    pst = PS[2 + half]
                                    for c4 in range(4):
                                        c = half * 4 + c4
                                        P.op("pe", lambda e, c=c, c4=c4, s4=s4, pst=pst, h=h: e.transpose(
                                            out=pst.t[:, c4 * 128:(c4 + 1) * 128],
                                            in_=h.t[:, c, s4 * 128:(s4 + 1) * 128],
                                            identity=ident.t[:]), reads=[h.regs[c], ident.r], writes=[pst.r])
                                    act(otm.r, otm.t[:, s4, half * 512:(half + 1) * 512], pst.t[:], AF.Copy, [pst.r])
                            store(otm.r, out_d[t * TS:(t + 1) * TS, :].rearrange("(s p) d -> p s d", p=128), otm.t[:],
                                  dsem_st[9])
                        else:
                            store(h.regs, dst_ha[t], h.t[:].rearrange("p c n -> p (c n)"), dsem_st[s3])
                    return [lambda d=d: chunk(d) for d in range(8)]

                for ti, t in enumerate(tiles):
                    s = ti % 2
                    s3 = ti % 3
                    r_ = mod_row(t)
                    h = hA[s3]
                    hmt = hmd[s]
                    def tile_loads(ti_):
                        t_ = tiles[ti_]
                        hm_ = hmd[ti_ % 2]
                        h_ = hA[ti_ % 3]
                        load(hm_.r, hm_.t[:].rearrange("p c n -> p (c n)"), HMID[t_], dsem_ld[2 + ti_ % 2])
                        load(h_.regs, h_.t[:].rearrange("p c n -> p (c n)"), src_ha[t_], dsem_ld[[0, 1, 12][ti_ % 3]])
                    if ti == 0:
                        tile_loads(0)
                        load_w_blocks(stg, W2, w_ffn_out[l, i // 2], DFF, [[(d * 128, 128)] for d in range(8)])
                        if len(tiles) > 1:
                            tile_loads(1)
                    elif ti + 1 < len(tiles):
                        tile_loads(ti + 1)
                    psS, psQ = (PS[4], PS[5]) if s == 0 else (PS[6], PS[7])
                    for d in range(8):
                        py = PS[d % 2]
                        for f in range(NFC):
                            mm(py, py.t[:], W2.t[:, f, d * 128:(d + 1) * 128], hmt.t[:, f, :], f == 0, f == NFC - 1,
                               [W2.regs[d], hmt.r])
                        dve(lambda e, d=d, py=py, h=h, r_=r_: e.scalar_tensor_tensor(
                            out=h.t[:, d, :], in0=py.t[:], scalar=GAT.t[:, l, i, d, r_:r_ + 1], in1=h.t[:, d, :],
                            op0=ALU.mult, op1=ALU.add), [py.r, h.regs[d], GAT.r], [h.regs[d]])
                        ln_stats(h.t[:, d, :], h.regs[d], d, xb, xsq, psS, psQ)
                        if tail_q:
                            tail_q.pop(0)()
                    while tail_q:
                        tail_q.pop(0)()
                    ln_finish(psS, psQ, means[s], rstds[s])
                    tail_q = make_tail(t, h, means[s], rstds[s], s3)
                while tail_q:
                    tail_q.pop(0)()
                P.barrier()

        def phase_mix_a(src_ha):
            l, i = 0, 1
            with ExitStack() as ph:
                stg = [sb(ph, [128, STG_N], F32) for _ in range(2)]
                W = sb(ph, [128, 8, 2048], BF16)
                load_w(stg, W, w_mix_in[0], D, 2048)
                Wsw = sb(ph, [128, 8, 1024], BF16)
                for k in range(8):
                    for hf in range(2):
                        dve(lambda e, k=k, hf=hf: e.tensor_copy(
                            out=Wsw.t[:, k, :].rearrange("p (b h f) -> p b h f", h=2, f=16)[:, :, hf, :],
                            in_=W.t[:, k, 0:1024].rearrange("p (b h f) -> p b h f", h=2, f=16)[:, :, 1 - hf, :]),
                            [W.r], [Wsw.r])
                cos = sb(ph, [128, TT], F32)
                sin = sb(ph, [128, TT], F32)
                load(cos.r, cos.t[:], cos_d, dsem_ld[4])
                load(sin.r, sin.t[:], sin_d, dsem_ld[5])
                hA = [sb(ph, [128, 8, TS], F32, 8) for _ in range(2)]
                xm = [sb(ph, [128, 8, TS], BF16, 8) for _ in range(2)]
                qo = [sb(ph, [128, 4, TS], BF16) for _ in range(2)]
                ko = [sb(ph, [128, 4, TS], BF16) for _ in range(2)]
                vt = [sb(ph, [128, 4, 512], BF16) for _ in range(2)]
                ut = [sb(ph, [128, 4, TS], F32) for _ in range(2)]
                ta = [sb(ph, [128, TS], F32) for _ in range(2)]
                tb = [sb(ph, [128, TS], F32) for _ in range(2)]
                zt = sb(ph, [128, 4, 8], F32)
                dve(lambda e: e.memset(zt.t[:], 0.0), [], [zt.r])
                for b in range(NB):
                    store(zt.r, U_d[b][:, :, 0:8], zt.t[:], dsem_st[8])
                    store(zt.r, U_d[b][:, :, TT + 8:TT + 16], zt.t[:], dsem_st[8])
                ci = 0
                for ti, t in enumerate(range(NTILE)):
                    s = ti % 2
                    lat = t < NTL
                    r_ = mod_row(t)
                    b, tl = (t // TPB, t % TPB) if lat else (0, 0)
                    t0 = tl * TS
                    h = hA[s]
                    load(h.regs, h.t[:].rearrange("p c n -> p (c n)"), src_ha[t], dsem_ld[s])
                    for c in range(8):
                        act(xm[s].regs[c], xm[s].t[:, c, :], h.t[:, c, :], AF.Identity, [h.regs[c], AIN.r, MOD[l].r],
                            bias=MOD[l].t[:, 3 * i, c, r_:r_ + 1], scale=AIN.t[:, l, i, c, r_:r_ + 1])
                    xr = xm[s].regs
                    for qk in ((0, 1) if lat else (1,)):
                        dst = (qo if qk == 0 else ko)[s]
                        for hh in range(4):
                            c0 = qk * 512 + hh * 128
                            p1, p2 = PS[ci % 2], PS[2 + ci % 2]
                            for k in range(8):
                                mm(p1, p1.t[:], W.t[:, k, c0:c0 + 128], xm[s].t[:, k, :], k == 0, k == 7, [W.r, xr[k]])
                            if lat:
                                for k in range(8):
                                    mm(p2, p2.t[:], Wsw.t[:, k, c0:c0 + 128], xm[s].t[:, k, :], k == 0, k == 7,
                                       [Wsw.r, xr[k]])
                                a_, b_ = ta[ci % 2], tb[ci % 2]
                                dve(lambda e, a_=a_, p1=p1, t0=t0: e.tensor_tensor(
                                    out=a_.t[:], in0=p1.t[:], in1=cos.t[:, t0:t0 + TS], op=ALU.mult),
                                    [p1.r, cos.r], [a_.r])
                                dve(lambda e, b_=b_, p2=p2, t0=t0: e.tensor_tensor(
                                    out=b_.t[:], in0=p2.t[:], in1=sin.t[:, t0:t0 + TS], op=ALU.mult),
                                    [p2.r, sin.r], [b_.r])
                                dve(lambda e, a_=a_, b_=b_, dst=dst, hh=hh: e.tensor_tensor(
                                    out=dst.t[:, hh, :], in0=a_.t[:], in1=b_.t[:], op=ALU.add),
                                    [a_.r, b_.r], [dst.r])
                            else:
                                act(dst.r, dst.t[:, hh, :], p1.t[:], AF.Copy, [p1.r])
                            ci += 1
                    if lat:
                        store(qo[s].r, QT_d[t], qo[s].t[:].rearrange("p h n -> p (h n)"), dsem_st[s])
                        store(ko[s].r, KT_d[b][:, :, CTX + t0:CTX + t0 + TS], ko[s].t[:], dsem_st[2 + s])
                    else:
                        for bb in range(NB):
                            store(ko[s].r, KT_d[bb][:, :, 0:CTX], ko[s].t[:, :, bb * CTX:(bb + 1) * CTX], dsem_st[2 + s])
                    for s4 in range(4):
                        pv = PS[4 + s4 % 2]
                        for k in range(8):
                            mm(pv, pv.t[:], xm[s].t[:, k, s4 * 128:(s4 + 1) * 128], W.t[:, k, 1024:1536],
                               k == 0, k == 7, [W.r, xr[k]])
                        act(vt[s].r, vt[s].t[:, s4, :], pv.t[:], AF.Copy, [pv.r])
                    if lat:
                        store(vt[s].r, V_d[b][:, 2 + tl * 4:2 + tl * 4 + 4, :], vt[s].t[:], dsem_st[4 + s])
                        for g in range(4):
                            pu = PS[6 + g % 2]
                            for k in range(8):
                                mm(pu, pu.t[:], W.t[:, k, 1536 + g * 128:1536 + (g + 1) * 128], xm[s].t[:, k, :],
                                   k == 0, k == 7, [W.r, xr[k]])
                            act(ut[s].r, ut[s].t[:, g, :], pu.t[:], AF.Copy, [pu.r])
                        store(ut[s].r, U_d[b][:, :, 8 + t0:8 + t0 + TS], ut[s].t[:], dsem_st[6 + s])
                    else:
                        for bb in range(NB):
                            store(vt[s].r, V_d[bb][:, 0:2, :], vt[s].t[:, 2 * bb:2 * bb + 2, :], dsem_st[4 + s])
                P.barrier()

        def phase_attn(src_ha, dst_ha):
            l, i = 0, 1
            li = (l * 3 + i) * 8
            with ExitStack() as ph:
                stg = [sb(ph, [128, 1024], F32) for _ in range(2)]
                Wo = sb(ph, [128, 8, D], BF16)
                load_w(stg, Wo, w_mix_out[0], D, D, stg_n=1024)
                Wp = sb(ph, [128, 4, 128], BF16)
                load_w(stg, Wp, w_pool[0].rearrange("g c d -> (g c) d"), 512, 128, stg_n=1024)
                KT = sb(ph, [128, 4, NKEY], BF16)
                V = sb(ph, [128, NKC, 512], BF16)
                qt = [sb(ph, [128, 4, TS], BF16) for _ in range(2)]
                ut1 = sb(ph, [128, 4, TS + 16], F32)
                ut = [ut1, ut1]
                rc1 = sb(ph, [128, 4, TS], F32)
                rc = [rc1, rc1]
                hA = [sb(ph, [128, 8, TS], F32, 8) for _ in range(2)]
                E = [sb(ph, [128, TS], BF16) for _ in range(4)]
                mg = sb(ph, [128, 8, TS], BF16, 8)
                r1 = sb(ph, [128, TS], F32)
                r2 = sb(ph, [128, TS], F32)
                oa = sb(ph, [128, TS], F32)
                ob = sb(ph, [128, TS], F32)
                osq = sb(ph, [128, TS], BF16)
                pw = [sb(ph, [128, TS + 16], F32) for _ in range(2)]
                pb = [sb(ph, [128, TS], BF16) for _ in range(4)]
                xb = sb(ph, [128, 2, TS], BF16, 2)
                xsq = sb(ph, [128, 2, TS], BF16, 2)
                means = [sb(ph, [128, TS], F32) for _ in range(2)]
                rstds = [sb(ph, [128, TS], F32) for _ in range(2)]
                tail_q = []
                cs = 0
                ce = 0
                for b in range(NB):
                    load(KT.r, KT.t[:], KT_d[b], dsem_ld[4])
                    load(V.r, V.t[:], V_d[b], dsem_ld[5])
                    for tl in range(TPB):
                        t = b * TPB + tl
                        s = t % 2
                        t0 = tl * TS
                        h = hA[s]
                        load(qt[s].r, qt[s].t[:].rearrange("p h n -> p (h n)"), QT_d[t], dsem_ld[2 + s])
                        load(ut[s].r, ut[s].t[:], U_d[b][:, :, t0:t0 + TS + 16], dsem_ld[6])
                        load(rc[s].r, rc[s].t[:], rcnt_d[:, :, t0:t0 + TS], dsem_ld[10])
                        load(h.regs, h.t[:].rearrange("p c n -> p (c n)"), src_ha[t], dsem_ld[s])
                        def pool_dve(s=s):
                            for g, w in enumerate(POOL_W):
                                cur, cw, cr = ut[s].t[:, g, :], TS + 16, ut[s].r
                                step = 1
                                pi = 0
                                while step < w:
                                    nw = cw - step
                                    o_ = pw[pi % 2]
                                    dve(lambda e, o_=o_, cur=cur, nw=nw, step=step: e.tensor_tensor(
                                        out=o_.t[:, 0:nw], in0=cur[:, 0:nw], in1=cur[:, step:step + nw], op=ALU.add),
                                        [cr], [o_.r])
                                    cur, cw, cr = o_.t, nw, o_.r
                                    step *= 2
                                    pi += 1
                                o_ = pw[pi % 2]
                                off = 8 - w // 2
                                dve(lambda e, o_=o_, cur=cur, off=off, g=g, s=s: e.tensor_tensor(
                                    out=o_.t[:, 0:TS], in0=cur[:, off:off + TS], in1=rc[s].t[:, g, :], op=ALU.mult),
                                    [cr, rc[s].r], [o_.r])
                                pbt = pb[g]
                                dve(lambda e, o_=o_, pbt=pbt, g=g, s=s: e.tensor_tensor(
                                    out=pbt.t[:], in0=o_.t[:, 0:TS], in1=ut[s].t[:, g, 8:8 + TS], op=ALU.subtract),
                                    [o_.r, ut[s].r], [pbt.r])

                        def pool_mm():
                            nonlocal cs
                            for g in range(4):
                                R = PS[cs % 4]
                                cs += 1
                                pbt = pb[g]
                                mm(R, R.t[:], Wp.t[:, g, :], pbt.t[:], True, True, [Wp.r, pbt.r])
                                act(mg.regs[4 + g], mg.t[:, 4 + g, :], R.t[:], AF.Identity, [R.r, colsC.r],
                                    scale=colsC.t[:, g:g + 1])

                        O1, O2, Z1, Z2 = PS[4], PS[5], PS[6], PS[7]
                        if "noheads" in str(debug):
                            for c8 in range(8):
                                dve(lambda e, c8=c8: e.memset(mg.t[:, c8, :], 0.0), [], [mg.regs[c8]])
                        pending = []
                        pool_done = False
                        for hh in range(0 if "noheads" in str(debug) else 4):
                            def emit_S(kc, hh=hh, s=s):
                                nonlocal cs, ce
                                S1, S2 = PS[cs % 4], PS[(cs + 1) % 4]
                                cs += 2
                                ksl = slice(kc * 128, (kc + 1) * 128)
                                mm(S1, S1.t[:], KT.t[0:64, hh, ksl], qt[s].t[0:64, hh, :], True, True, [KT.r, qt[s].r])
                                mm(S2, S2.t[:], KT.t[64:128, hh, ksl], qt[s].t[64:128, hh, :], True, True,
                                   [KT.r, qt[s].r])
                                E1, E2 = E[ce % 4], E[(ce + 1) % 4]
                                ce += 2
                                act(E1.r, E1.t[:], S1.t[:], AF.Exp, [S1.r], scale=0.125)
                                act(E2.r, E2.t[:], S2.t[:], AF.Exp, [S2.r], scale=0.125)
                                return E1, E2

                            def emit_O(kc, E1, E2, hh=hh):
                                st_, sp_ = kc == 0, kc == NKC - 1
                                vsl = V.t[:, kc, hh * 128:(hh + 1) * 128]
                                mm(O1, O1.t[:], vsl, E1.t[:], st_, sp_, [V.r, E1.r])
                                mm(Z1, Z1.t[:], ones1.t[:], E1.t[:], st_, sp_, [ones1.r, E1.r])
                                mm(O2, O2.t[:], vsl, E2.t[:], st_, sp_, [V.r, E2.r])
                                mm(Z2, Z2.t[:], ones1.t[:], E2.t[:], st_, sp_, [ones1.r, E2.r])

                            prev = emit_S(0)
                            for kc in range(NKC):
                                nxt = emit_S(kc + 1) if kc + 1 < NKC else None
                                emit_O(kc, *prev)
                                prev = nxt
                                if pending and kc in (1, 3, 5):
                                    pending.pop(0)()
                                if hh == 1 and kc == 7:
                                    pool_dve()
                                    pool_done = True
                                if hh == 0 and tail_q and kc >= 8:
                                    tail_q.pop(0)()
                            while pending:
                                pending.pop(0)()
                            if hh == 0:
                                while tail_q:
                                    tail_q.pop(0)()
                            if hh == 1 and not pool_done:
                                pool_dve()
                            act(oa.r, oa.t[:], O1.t[:], AF.Copy, [O1.r])
                            dve(lambda e: e.tensor_copy(out=ob.t[:], in_=O2.t[:]), [O2.r], [ob.r])
                            act(r1.r, r1.t[:], Z1.t[:], AF.Ln, [Z1.r])
                            act(r2.r, r2.t[:], Z2.t[:], AF.Ln, [Z2.r])

                            def stB():
                                act(r1.r, r1.t[:], r1.t[:], AF.Exp, [r1.r], scale=-1.0)
                                act(r2.r, r2.t[:], r2.t[:], AF.Exp, [r2.r], scale=-1.0)
                                dve(lambda e: e.tensor_tensor(out=oa.t[:], in0=oa.t[:], in1=r1.t[:], op=ALU.mult),
                                    [oa.r, r1.r], [oa.r])
                                dve(lambda e: e.tensor_tensor(out=ob.t[:], in0=ob.t[:], in1=r2.t[:], op=ALU.mult),
                                    [ob.r, r2.r], [ob.r])
                                dve(lambda e: e.scalar_tensor_tensor(out=oa.t[:], in0=ob.t[:], scalar=lamc.t[:, 0:1],
                                                                     in1=oa.t[:], op0=ALU.mult, op1=ALU.add),
                                    [ob.r, oa.r, lamc.r], [oa.r])
                                act(osq.r, osq.t[:], oa.t[:], AF.Square, [oa.r])

                            def stC(Rb=None):
                                nonlocal cs
                                if Rb is None:
                                    R = PS[cs % 4]
                                    cs += 1
                                else:
                                    R = Rb
                                mm(R, R.t[:], ones128.t[:], osq.t[:], True, True, [ones128.r, osq.r])
                                dve(lambda e, R=R: e.tensor_scalar(out=r1.t[:], in0=R.t[:], scalar1=EPS, scalar2=None,
                                                                   op0=ALU.add), [R.r], [r1.r])
                                act(r1.r, r1.t[:], r1.t[:], AF.Ln, [r1.r])

                            def stD(hh=hh):
                                act(r1.r, r1.t[:], r1.t[:], AF.Exp, [r1.r], scale=-0.5)
                                dve(lambda e: e.tensor_tensor(out=oa.t[:], in0=oa.t[:], in1=r1.t[:], op=ALU.mult),
                                    [oa.r, r1.r], [oa.r])
                                act(mg.regs[hh], mg.t[:, hh, :], oa.t[:], AF.Identity, [oa.r, sublc.r],
                                    scale=sublc.t[:, 0:1])

                            pending = [stB, stC, stD]
                            if hh == 3:
                                ybanks = [PS[0], PS[1], PS[2], PS[3], PS[6]]
                                for d in range(5):
                                    py = ybanks[d]
                                    for j, k in enumerate((0, 1, 2, 4, 5, 6, 7)):
                                        mm(py, py.t[:], Wo.t[:, k, d * 128:(d + 1) * 128], mg.t[:, k, :], j == 0, False,
                                           [Wo.r, mg.regs[k]])
                                stB()
                                stC(PS[7])
                                stD()
                                pending = []
                            if hh == 2:
                                pool_mm()
                        if debug:
                            store(mg.regs, MG_d[t], mg.t[:].rearrange("p c n -> p (c n)"), dsem_st[9])
                        psS, psQ = PS[4], PS[5]
                        for d in range(8):
                            if d < 5:
                                py = ybanks[d]
                                mm(py, py.t[:], Wo.t[:, 3, d * 128:(d + 1) * 128], mg.t[:, 3, :], False, True,
                                   [Wo.r, mg.regs[3]])
                            else:
                                py = ybanks[d - 5]
                                for k in range(8):
                                    mm(py, py.t[:], Wo.t[:, k, d * 128:(d + 1) * 128], mg.t[:, k, :], k == 0, k == 7,
                                       [Wo.r, mg.regs[k]])
                            dve(lambda e, d=d, py=py, h=h, b=b: e.scalar_tensor_tensor(
                                out=h.t[:, d, :], in0=py.t[:], scalar=GAT.t[:, l, i, d, b:b + 1], in1=h.t[:, d, :],
                                op0=ALU.mult, op1=ALU.add), [py.r, h.regs[d], GAT.r], [h.regs[d]])
                            ln_stats(h.t[:, d, :], h.regs[d], d, xb, xsq, psS, psQ)
                        mean, rstd = means[s], rstds[s]
                        ln_finish(psS, psQ, mean, rstd)

                        def make_tail(t=t, h=h, mean=mean, rstd=rstd, s=s):
                            def chunk(d):
                                ln_norm(h.t[:, d, :], h.regs[d], mean, rstd)
                                act(h.regs[d], h.t[:, d, :], h.t[:, d, :], AF.Identity, [h.regs[d], LNGA.r, LNBA.r],
                                    bias=LNBA.t[:, li + d:li + d + 1], scale=LNGA.t[:, li + d:li + d + 1])
                                if d == 7:
                                    store(h.regs, dst_ha[t], h.t[:].rearrange("p c n -> p (c n)"), dsem_st[s])
                            return [lambda d=d: chunk(d) for d in range(8)]
                        tail_q = make_tail()
                while tail_q:
                    tail_q.pop(0)()
                P.barrier()

        def phase_conv_a(src_ha):
            l, i = 1, 1
            with ExitStack() as ph:
                stg = [sb(ph, [128, STG_N], F32) for _ in range(2)]
                W = sb(ph, [128, 8, 2048], BF16, 8)
                hA = [sb(ph, [128, 8, TS], F32, 8) for _ in range(2)]
                load(hA[0].regs, hA[0].t[:].rearrange("p c n -> p (c n)"), src_ha[0], dsem_ld[0])
                load_w_blocks(stg, W, w_c1[0], D, [[(c * 128, 128), (D + c * 128, 128)] for c in range(8)])
                xm = [sb(ph, [128, 8, TS], BF16, 8) for _ in range(2)]
                zo = [sb(ph, [128, 8, TS], BF16) for _ in range(2)]
                sg = [sb(ph, [128, TS], F32) for _ in range(2)]
                zt = sb(ph, [128, 8, 16], BF16)
                dve(lambda e: e.memset(zt.t[:], 0.0), [], [zt.r])
                for b in range(NB):
                    store(zt.r, ZC_d[b][:, :, 0:16], zt.t[:], dsem_st[8])
                    store(zt.r, ZC_d[b][:, :, TT + 16:TT + 32], zt.t[:], dsem_st[8])
                for t in range(NTL):
                    s = t % 2
                    b, tl = t // TPB, t % TPB
                    t0 = tl * TS
                    h = hA[s]
                    if t > 0:
                        load(h.regs, h.t[:].rearrange("p c n -> p (c n)"), src_ha[t], dsem_ld[s])
                    for c in range(8):
                        act(xm[s].regs[c], xm[s].t[:, c, :], h.t[:, c, :], AF.Identity, [h.regs[c], AIN.r, MOD[l].r],
                            bias=MOD[l].t[:, 3 * i, c, b:b + 1], scale=AIN.t[:, l, i, c, b:b + 1])
                    for c in range(8):
                        pa, pg = PS[c % 2], PS[2 + c % 2]
                        for k in range(8):
                            mm(pa, pa.t[:], W.t[:, k, c * 128:(c + 1) * 128], xm[s].t[:, k, :], k == 0, k == 7,
                               [W.regs[c], xm[s].regs[k]])
                        for k in range(8):
                            mm(pg, pg.t[:], W.t[:, k, D + c * 128:D + (c + 1) * 128], xm[s].t[:, k, :], k == 0, k == 7,
                               [W.regs[c], xm[s].regs[k]])
                        sgt = sg[c % 2]
                        act(sgt.r, sgt.t[:], pg.t[:], AF.Sigmoid, [pg.r, colsC.r], bias=colsC.t[:, 12 + c:13 + c])
                        dve(lambda e, c=c, pa=pa, sgt=sgt, s=s: e.scalar_tensor_tensor(
                            out=zo[s].t[:, c, :], in0=pa.t[:], scalar=colsC.t[:, 4 + c:5 + c], in1=sgt.t[:],
                            op0=ALU.add, op1=ALU.mult), [pa.r, sgt.r, colsC.r], [zo[s].r])
                    store(zo[s].r, ZC_d[b][:, :, 16 + t0:16 + t0 + TS], zo[s].t[:], dsem_st[s])
                P.barrier()

        def phase_conv_b(src_ha, dst_ha):
            l, i = 1, 1
            li = (l * 3 + i) * 8
            with ExitStack() as ph:
                stg = [sb(ph, [128, STG_N], F32) for _ in range(2)]
                Wc2 = sb(ph, [128, 8, D], BF16)
                DG = sb(ph, [128, CW * 8, 128], BF16, 8)
                for c_ in range(8):
                    for j_ in range(CW):
                        idx = j_ * 8 + c_
                        col = colsD.t[:, idx:idx + 1] if idx < 124 else colsE.t[:, idx - 124:idx - 123]
                        dve(lambda e, idx=idx, col=col: e.tensor_scalar(out=DG.t[:, idx, :], in0=ident.t[:],
                                                                        scalar1=col, scalar2=None, op0=ALU.mult),
                            [ident.r, colsD.r, colsE.r], [DG.regs[c_]])
                hA = [sb(ph, [128, 8, TS], F32, 8) for _ in range(2)]
                zt = [sb(ph, [128, 8, TS + 32], BF16) for _ in range(2)]
                cvs = [sb(ph, [128, 8, TS], F32, 8) for _ in range(2)]
                sc = sb(ph, [128, 8, TS], BF16, 8)
                xb = sb(ph, [128, 2, TS], BF16, 2)
                xsq = sb(ph, [128, 2, TS], BF16, 2)
                mean = sb(ph, [128, TS], F32)
                rstd = sb(ph, [128, TS], F32)
                mean2 = sb(ph, [128, TS], F32)
                rstd2 = sb(ph, [128, TS], F32)

                def issue_loads(t):
                    s = t % 2
                    b, tl = t // TPB, t % TPB
                    t0 = tl * TS
                    load(zt[s].r, zt[s].t[:], ZC_d[b][:, :, t0:t0 + TS + 32], dsem_ld[2 + s])
                    load(hA[s].regs, hA[s].t[:].rearrange("p c n -> p (c n)"), src_ha[t], dsem_ld[s])

                def conv(t):
                    s = t % 2
                    cv = cvs[s]
                    for c in range(8):
                        pc = PS[c % 2]
                        for j in range(CW):
                            mm(pc, pc.t[:], DG.t[:, j * 8 + c, :], zt[s].t[:, c, 1 + j:1 + j + TS], j == 0, j == CW - 1,
                               [DG.regs[c], zt[s].r])
                        act(cv.regs[c], cv.t[:, c, :], pc.t[:], AF.Identity, [pc.r, colsC.r],
                            bias=colsC.t[:, 20 + c:21 + c])
                        ln_stats(cv.t[:, c, :], cv.regs[c], c, xb, xsq, PS[4], PS[5])

                def ln1chain(t):
                    cv = cvs[t % 2]
                    ln_finish(PS[4], PS[5], mean, rstd)
                    for c in range(8):
                        ln_norm(cv.t[:, c, :], cv.regs[c], mean, rstd)
                        act(sc.regs[c], sc.t[:, c, :], cv.t[:, c, :], AF.Silu, [cv.regs[c], colsC.r],
                            bias=colsC.t[:, 36 + c:37 + c], scale=colsC.t[:, 28 + c:29 + c])

                def c2(t):
                    s = t % 2
                    b = t // TPB
                    h = hA[s]
                    for d in range(8):
                        py = PS[2 + d % 2]
                        for c in range(8):
                            mm(py, py.t[:], Wc2.t[:, c, d * 128:(d + 1) * 128], sc.t[:, c, :], c == 0, c == 7,
                               [Wc2.r, sc.regs[c]])
                        dve(lambda e, d=d, h=h, b=b: e.tensor_scalar(out=h.t[:, d, :], in0=h.t[:, d, :],
                                                                     scalar1=bc2g.t[:, d, b:b + 1], scalar2=None,
                                                                     op0=ALU.add), [h.regs[d], bc2g.r], [h.regs[d]])
                        dve(lambda e, d=d, py=py, h=h, b=b: e.scalar_tensor_tensor(
                            out=h.t[:, d, :], in0=py.t[:], scalar=GAT.t[:, l, i, d, b:b + 1], in1=h.t[:, d, :],
                            op0=ALU.mult, op1=ALU.add), [py.r, h.regs[d], GAT.r], [h.regs[d]])
                        ln_stats(h.t[:, d, :], h.regs[d], d, xb, xsq, PS[6], PS[7])
                    ln_finish(PS[6], PS[7], mean2, rstd2)
                    for d in range(8):
                        ln_norm(h.t[:, d, :], h.regs[d], mean2, rstd2)
                        act(h.regs[d], h.t[:, d, :], h.t[:, d, :], AF.Identity, [h.regs[d], LNGA.r, LNBA.r],
                            bias=LNBA.t[:, li + d:li + d + 1], scale=LNGA.t[:, li + d:li + d + 1])
                    store(h.regs, dst_ha[t], h.t[:].rearrange("p c n -> p (c n)"), dsem_st[s])

                issue_loads(0)
                load_w(stg, Wc2, w_c2[0], D, D)
                conv(0)
                for t in range(NTL):
                    ln1chain(t)
                    if t + 1 < NTL:
                        issue_loads(t + 1)
                        conv(t + 1)
                    c2(t)
                P.barrier()

        phase_ffn_a(0, 0, None, HA[0], tiles=list(range(NTILE)))
        phase_ffn_b(0, 0, HA[0], HA[1], tiles=list(range(NTILE)))
        if debug == "ffn":
            phase_ffn_a(1, 2, HA[1])
            phase_ffn_b(1, 2, HA[1], None, final=True)
        else:
            phase_mix_a(HA[1])
            if debug != "mixa":
                phase_attn(HA[1], HA[0])
            if not debug or debug == "full":
                phase_ffn_a(0, 2, HA[0])
                phase_ffn_b(0, 2, HA[0], HA[1])
                phase_ffn_a(1, 0, HA[1])
                phase_ffn_b(1, 0, HA[1], HA[0])
                phase_conv_a(HA[0])
                phase_conv_b(HA[0], HA[1])
                phase_ffn_a(1, 2, HA[1])
                phase_ffn_b(1, 2, HA[1], None, final=True)

        P.barrier()

        with nc.Block() as block:
            @block.tensor
            def _(e):
                P.replay("pe", e)

            @block.scalar
            def _(e):
                P.replay("act", e)

            @block.vector
            def _(e):
                P.replay("dve", e)

            @block.gpsimd
            def _(e):
                P.replay("pool", e)

            @block.sync
            def _(e):
                P.replay("sp", e)
    return nc


def host_consts(TT):
    ident = np.eye(128, dtype=np.float32)
    p = np.arange(128)
    r = p % 64
    axis = r // 32
    half = (r % 32) // 16
    f = r % 16
    inv_freq = (np.float32(10000.0) ** (-np.arange(16, dtype=np.float32) / np.float32(16))).astype(np.float32)
    t = np.arange(TT)
    row = (t // 64).astype(np.float32)
    col = (t % 64).astype(np.float32)
    pos = np.where(axis[:, None] == 0, row[None, :], col[None, :]).astype(np.float32)
    ang = (pos * inv_freq[f][:, None]).astype(np.float32)
    cos = np.cos(ang).astype(np.float32)
    sin = np.sin(ang).astype(np.float32)
    sins = np.where(half[:, None] == 0, -sin, sin).astype(np.float32)
    rc = np.zeros((4, TT), np.float32)
    for g, w in enumerate(POOL_W):
        lo = np.clip(t - w // 2, 0, TT)
        hi = np.clip(t - w // 2 + w, 0, TT)
        rc[g] = (1.0 / (hi - lo)).astype(np.float32)
    rcnt = np.ascontiguousarray(np.broadcast_to(rc[None], (128, 4, TT))).astype(np.float32)
    return {"ident": ident, "rope_cos": cos, "rope_sin": sins, "rcnt": rcnt}


_W_KEYS = ("w_ada", "b_ada", "ln_g", "ln_b", "w_ffn_in", "w_ffn_out", "w_mix_in", "w_mix_out", "lam_q1",
           "lam_k1", "lam_q2", "lam_k2", "subln_g", "w_pool", "pool_scale", "w_c1", "b_c1", "w_dw", "b_dw",
           "conv_ln_g", "conv_ln_b", "w_c2", "b_c2")


def make_in_maps(inputs, NB, TT, ncores):
    hc = host_consts(TT)
    maps = []
    for i in range(ncores):
        m = {k: np.ascontiguousarray(np.asarray(inputs[k], dtype=np.float32)) for k in _W_KEYS}
        xb = np.asarray(inputs["x"])[i * NB:(i + 1) * NB]
        m["x"] = np.ascontiguousarray(xb.reshape(NB * TT, D))
        m["ctx"] = np.ascontiguousarray(np.asarray(inputs["ctx"])[i * NB:(i + 1) * NB].reshape(NB * CTX, D))
        m["cc"] = np.ascontiguousarray(np.concatenate(
            [np.asarray(inputs["c"])[i * NB:(i + 1) * NB], np.asarray(inputs["c_ctx"])[None, :]], axis=0))
        m.update(hc)
        maps.append(m)
    return maps


def kernel(**inputs):
    B, TT, _ = inputs["x"].shape
    ncores = 8
    NB = B // ncores
    nc = build(NB, TT)
    maps = make_in_maps(inputs, NB, TT, ncores)
    res = run_bass_kernel_spmd(nc, maps, core_ids=list(range(ncores)))
    outs = [np.asarray(r["out"]).reshape(NB, TT, D) for r in res.results]
    return np.concatenate(outs, axis=0).astype(np.float32)
```
